# Optimizing a Trainium2 kernel written in Bass

```python
import jax, jax.numpy as jnp
from jax import lax
import numpy as np

D_MODEL = 1024
BATCH = 32
SEQ = 2048
DEPTH = 2
DEC_BATCH = 16
DEC_SEQ = 4096
PAST_LEN = 128

N_MIXERS = 2
N_Q_HEADS = 8
N_KV_HEADS = 2
HEAD_DIM = 128
GQA_GROUP = N_Q_HEADS // N_KV_HEADS
QKV_DIM = (N_Q_HEADS + 2 * N_KV_HEADS) * HEAD_DIM
ROPE_HALF = HEAD_DIM // 2
ROPE_THETA = 10000.0
Q_BLOCK = 128
GRID_W = 64
GMLP_WIDTH = D_MODEL
GMLP_GROUPS = 8
CHUNK = 128
D_FF = 2816
PLE_DIM = 256
EPS = 1e-6
N_ATTN = (DEPTH + 1) // 2
N_GMLP = DEPTH // 2

kernel_name = "hybrid_gqa_gmlp_macaron_encoder"


def rms_norm(x, g):
    xf = x.astype(jnp.float32)
    y = xf * lax.rsqrt(jnp.mean(xf * xf, axis=-1, keepdims=True) + EPS)
    return (y * g.astype(jnp.float32)).astype(x.dtype)


def swiglu(x, w_gate, w_up, w_down):
    return (jax.nn.silu(x @ w_gate) * (x @ w_up)) @ w_down


def axial_angles(seq_len):
    rows = seq_len // GRID_W
    row = jnp.repeat(jnp.arange(rows, dtype=jnp.float32), GRID_W)
    col = jnp.tile(jnp.arange(GRID_W, dtype=jnp.float32), rows)
    inv = ROPE_THETA ** (-jnp.arange(0, ROPE_HALF, 2, dtype=jnp.float32) / ROPE_HALF)
    return row[:, None] * inv, col[:, None] * inv


def rope_1d(x, ang):
    c = jnp.cos(ang)[None, :, None, :]
    s = jnp.sin(ang)[None, :, None, :]
    x1, x2 = jnp.split(x, 2, axis=-1)
    return jnp.concatenate([x1 * c - x2 * s, x1 * s + x2 * c], axis=-1)


def axial_rope(x, ang_row, ang_col):
    xf = x.astype(jnp.float32)
    out = jnp.concatenate([rope_1d(xf[..., :ROPE_HALF], ang_row),
                           rope_1d(xf[..., ROPE_HALF:], ang_col)], axis=-1)
    return out.astype(x.dtype)


def attention_mixer(h, w_qkv, q_norm, k_norm, w_o):
    b, s, _ = h.shape
    qkv = h @ w_qkv
    q, k, v = jnp.split(qkv, [N_Q_HEADS * HEAD_DIM, (N_Q_HEADS + N_KV_HEADS) * HEAD_DIM], axis=-1)
    q = rms_norm(q.reshape(b, s, N_Q_HEADS, HEAD_DIM), q_norm)
    k = rms_norm(k.reshape(b, s, N_KV_HEADS, HEAD_DIM), k_norm)
    v = v.reshape(b, s, N_KV_HEADS, HEAD_DIM)
    ang_r, ang_c = axial_angles(s)
    q = axial_rope(q, ang_r, ang_c) * (HEAD_DIM ** -0.5)
    k = axial_rope(k, ang_r, ang_c)
    nb = s // Q_BLOCK
    qb = q.reshape(b, nb, Q_BLOCK, N_KV_HEADS, GQA_GROUP, HEAD_DIM).transpose(1, 0, 2, 3, 4, 5)

    def block(qi):
        sc = jnp.einsum('bqhgd,bkhd->bhgqk', qi, k, preferred_element_type=jnp.float32)
        pr = jax.nn.softmax(sc, axis=-1).astype(v.dtype)
        return jnp.einsum('bhgqk,bkhd->bqhgd', pr, v)

    o = lax.map(block, qb)
    o = o.transpose(1, 0, 2, 3, 4, 5).reshape(b, s, N_Q_HEADS * HEAD_DIM)
    return o @ w_o


def gmlp_mixer(h, w_in, v_norm, w_s, b_s, w_out):
    b, s, _ = h.shape
    z = jax.nn.gelu(h @ w_in, approximate=False)
    u, v = jnp.split(z, 2, axis=-1)
    v = rms_norm(v, v_norm)
    vc = v.reshape(b, s // CHUNK, CHUNK, GMLP_GROUPS, GMLP_WIDTH // GMLP_GROUPS)
    sv = jnp.einsum('gpq,bnqgc->bnpgc', w_s, vc) + b_s.T[None, None, :, :, None]
    return (u * sv.reshape(b, s, GMLP_WIDTH)) @ w_out


def trunk(x, p, ffn1_norm, ffn1_w_gate, ffn1_w_up, ffn1_w_down, mix_norm,
          attn_w_qkv, attn_q_norm, attn_k_norm, attn_w_o,
          gmlp_w_in, gmlp_v_norm, gmlp_w_s, gmlp_b_s, gmlp_w_out,
          ffn2_norm, ffn2_w_gate, ffn2_w_up, ffn2_w_down,
          ple_norm, ple_w_gate, ple_w_proj):
    h = x
    for i in range(DEPTH):
        h = h + 0.5 * swiglu(rms_norm(h, ffn1_norm[i]), ffn1_w_gate[i], ffn1_w_up[i], ffn1_w_down[i])
        hn = rms_norm(h, mix_norm[i])
        j = i // N_MIXERS
        if i % N_MIXERS == 0:
            h = h + attention_mixer(hn, attn_w_qkv[j], attn_q_norm[j], attn_k_norm[j], attn_w_o[j])
        else:
            h = h + gmlp_mixer(hn, gmlp_w_in[j], gmlp_v_norm[j], gmlp_w_s[j], gmlp_b_s[j], gmlp_w_out[j])
        h = h + 0.5 * swiglu(rms_norm(h, ffn2_norm[i]), ffn2_w_gate[i], ffn2_w_up[i], ffn2_w_down[i])
        gate = jax.nn.sigmoid(rms_norm(h, ple_norm[i]) @ ple_w_gate[i])
        h = h + gate * (p[i] @ ple_w_proj[i])
    return h


def setup_inputs(seed: int = 0) -> dict:
    key = jax.random.key(seed)
    ks = jax.random.split(key, 32)
    f32 = jnp.float32

    def nrm(k, shape, scale):
        return jax.random.normal(k, shape, dtype=f32) * scale

    def gain(k, shape):
        return 1.0 + 0.01 * jax.random.normal(k, shape, dtype=f32)

    return {
        "x_prompt": nrm(ks[0], (BATCH, SEQ, D_MODEL), 1.0),
        "x_sample": nrm(ks[1], (DEC_BATCH, DEC_SEQ, D_MODEL), 1.0),
        "p_prompt": nrm(ks[2], (DEPTH, BATCH, SEQ, PLE_DIM), 1.0),
        "p_sample": nrm(ks[3], (DEPTH, DEC_BATCH, DEC_SEQ, PLE_DIM), 1.0),
        "ffn1_norm": gain(ks[4], (DEPTH, D_MODEL)),
        "ffn1_w_gate": nrm(ks[5], (DEPTH, D_MODEL, D_FF), D_MODEL ** -0.5),
        "ffn1_w_up": nrm(ks[6], (DEPTH, D_MODEL, D_FF), D_MODEL ** -0.5),
        "ffn1_w_down": nrm(ks[7], (DEPTH, D_FF, D_MODEL), D_FF ** -0.5),
        "mix_norm": gain(ks[8], (DEPTH, D_MODEL)),
        "attn_w_qkv": nrm(ks[9], (N_ATTN, D_MODEL, QKV_DIM), D_MODEL ** -0.5),
        "attn_q_norm": gain(ks[10], (N_ATTN, HEAD_DIM)),
        "attn_k_norm": gain(ks[11], (N_ATTN, HEAD_DIM)),
        "attn_w_o": nrm(ks[12], (N_ATTN, N_Q_HEADS * HEAD_DIM, D_MODEL), (N_Q_HEADS * HEAD_DIM) ** -0.5),
        "gmlp_w_in": nrm(ks[13], (N_GMLP, D_MODEL, 2 * GMLP_WIDTH), D_MODEL ** -0.5),
        "gmlp_v_norm": gain(ks[14], (N_GMLP, GMLP_WIDTH)),
        "gmlp_w_s": nrm(ks[15], (N_GMLP, GMLP_GROUPS, CHUNK, CHUNK), CHUNK ** -0.5),
        "gmlp_b_s": 1.0 + 0.1 * jax.random.normal(ks[16], (N_GMLP, GMLP_GROUPS, CHUNK), dtype=f32),
        "gmlp_w_out": nrm(ks[17], (N_GMLP, GMLP_WIDTH, D_MODEL), GMLP_WIDTH ** -0.5),
        "ffn2_norm": gain(ks[18], (DEPTH, D_MODEL)),
        "ffn2_w_gate": nrm(ks[19], (DEPTH, D_MODEL, D_FF), D_MODEL ** -0.5),
        "ffn2_w_up": nrm(ks[20], (DEPTH, D_MODEL, D_FF), D_MODEL ** -0.5),
        "ffn2_w_down": nrm(ks[21], (DEPTH, D_FF, D_MODEL), D_FF ** -0.5),
        "ple_norm": gain(ks[22], (DEPTH, D_MODEL)),
        "ple_w_gate": nrm(ks[23], (DEPTH, D_MODEL, D_MODEL), D_MODEL ** -0.5),
        "ple_w_proj": nrm(ks[24], (DEPTH, PLE_DIM, D_MODEL), PLE_DIM ** -0.5),
    }


def reference(x_prompt, x_sample, p_prompt, p_sample,
              ffn1_norm, ffn1_w_gate, ffn1_w_up, ffn1_w_down, mix_norm,
              attn_w_qkv, attn_q_norm, attn_k_norm, attn_w_o,
              gmlp_w_in, gmlp_v_norm, gmlp_w_s, gmlp_b_s, gmlp_w_out,
              ffn2_norm, ffn2_w_gate, ffn2_w_up, ffn2_w_down,
              ple_norm, ple_w_gate, ple_w_proj):
    y_prompt = trunk(x_prompt, p_prompt, ffn1_norm, ffn1_w_gate, ffn1_w_up, ffn1_w_down, mix_norm,
                     attn_w_qkv, attn_q_norm, attn_k_norm, attn_w_o,
                     gmlp_w_in, gmlp_v_norm, gmlp_w_s, gmlp_b_s, gmlp_w_out,
                     ffn2_norm, ffn2_w_gate, ffn2_w_up, ffn2_w_down,
                     ple_norm, ple_w_gate, ple_w_proj)
    y_sample = trunk(x_sample, p_sample, ffn1_norm, ffn1_w_gate, ffn1_w_up, ffn1_w_down, mix_norm,
                     attn_w_qkv, attn_q_norm, attn_k_norm, attn_w_o,
                     gmlp_w_in, gmlp_v_norm, gmlp_w_s, gmlp_b_s, gmlp_w_out,
                     ffn2_norm, ffn2_w_gate, ffn2_w_up, ffn2_w_down,
                     ple_norm, ple_w_gate, ple_w_proj)
    return (y_prompt, y_sample)
```

```python
import numpy as np
import concourse.bass as bass
import concourse.mybir as mybir
from concourse.bass_utils import run_bass_kernel_spmd

F32 = mybir.dt.float32
BF16 = mybir.dt.bfloat16
AF = mybir.ActivationFunctionType
ALU = mybir.AluOpType

D = 1024
DFF = 2816
NJ = 22
KC = 8
T = 512
HD = 128
NQH = 8
NKV = 2
PLE = 256
EPS = 1e-6
NCORES = 8
SP_LEN = 2048
SS_LEN = 4096
SMAX = 4096
RING = 4
SLOT_ELEMS = 4096
HW = 256


class Sem:
    def __init__(self, nc, name, owner=None):
        self.h = nc.alloc_semaphore(name)
        self.owner = owner
        self.count = 0


class Eng:
    def __init__(self, nc, name, handle, is_pe=False):
        self.name = name
        self.h = handle
        self.is_pe = is_pe
        self.sem = Sem(nc, "s_" + name, owner=self)
        self.seen = {}


class Buf:
    def __init__(self, name, t=None, nslots=1, psum=False, dsem=None):
        self.name = name
        self.t = t
        self.n = nslots
        self.psum = psum
        self.dsem = dsem
        self.gen = 0
        self.w = [None] * nslots
        self.r = [dict() for _ in range(nslots)]

    def slots(self, sl):
        if sl is None:
            return range(self.n)
        if isinstance(sl, int):
            return (sl,)
        return sl


class Builder:
    def __init__(self, nps, nss, sp_len=SP_LEN, ss_len=SS_LEN):
        self.nps, self.nss, self.sp_len, self.ss_len = nps, nss, sp_len, ss_len
        nc = bass.Bass("TRN2", target_bir_lowering=False)
        self.nc = nc
        self.PE = Eng(nc, "pe", nc.tensor, is_pe=True)
        self.ACT = Eng(nc, "act", nc.scalar)
        self.DVE = Eng(nc, "dve", nc.vector)
        self.POOL = Eng(nc, "pool", nc.gpsimd)
        self.SP = Eng(nc, "sp", nc.sync)
        self.deferred = []
        self.unit_ctr = 0
        self.defer_at = 0
        self._declare_dram()
        self._alloc()

    def emit(self, E, fn, reads=(), writes=(), dsem=None):
        raw, oth = {}, {}

        def add(d, dep):
            if dep is None:
                return
            s, v = dep
            if d.get(s, 0) < v:
                d[s] = v

        for b, sl in reads:
            for i in b.slots(sl):
                add(raw, b.w[i])
                if b.psum:
                    for dep in b.r[i].items():
                        add(oth, dep)
        for b, sl in writes:
            for i in b.slots(sl):
                add(oth, b.w[i])
                for dep in b.r[i].items():
                    add(oth, dep)
        need = dict(oth)
        for s, v in raw.items():
            if need.get(s, 0) < v:
                need[s] = v
        for s, v in need.items():
            if s.owner is E and E.is_pe:
                continue
            if E.seen.get(s, 0) >= v:
                continue
            E.h.wait_ge(s.h, v)
            E.seen[s] = v
        ins = fn()
        if dsem is not None:
            dsem.count += 16
            ins.then_inc(dsem.h, 16)
            dep = (dsem, dsem.count)
        else:
            E.sem.count += 1
            ins.then_inc(E.sem.h, 1)
            dep = (E.sem, E.sem.count)
        for b, sl in reads:
            for i in b.slots(sl):
                if b.r[i].get(dep[0], 0) < dep[1]:
                    b.r[i][dep[0]] = dep[1]
        for b, sl in writes:
            for i in b.slots(sl):
                b.w[i] = dep
                b.r[i] = {}
        return dep

    def _declare_dram(self):
        nc = self.nc
        I = lambda n, s: nc.dram_tensor(n, list(s), F32, kind="ExternalInput").ap()
        self.xp = I("xp", (self.nps, self.sp_len, D)) if self.nps else None
        self.xs = I("xs", (self.nss, self.ss_len, D)) if self.nss else None
        self.pp = I("pp", (2, self.nps, self.sp_len, PLE)) if self.nps else None
        self.psm = I("psm", (2, self.nss, self.ss_len, PLE)) if self.nss else None
        self.w_in = {}
        shapes = {
            "ffn1_norm": (2, D), "ffn1_w_gate": (2, D, DFF), "ffn1_w_up": (2, D, DFF), "ffn1_w_down": (2, DFF, D),
            "mix_norm": (2, D), "attn_w_qkv": (1, D, 1536), "attn_q_norm": (1, HD), "attn_k_norm": (1, HD),
            "attn_w_o": (1, D, D), "gmlp_w_in": (1, D, 2 * D), "gmlp_v_norm": (1, D), "gmlp_w_s": (1, 8, 128, 128),
            "gmlp_b_s": (1, 8, 128), "gmlp_w_out": (1, D, D), "ffn2_norm": (2, D), "ffn2_w_gate": (2, D, DFF),
            "ffn2_w_up": (2, D, DFF), "ffn2_w_down": (2, DFF, D), "ple_norm": (2, D), "ple_w_gate": (2, D, D),
            "ple_w_proj": (2, PLE, D),
            "c_cos": (128, SMAX), "c_sin": (128, SMAX), "c_ident": (128, 128), "c_pswap": (128, 128),
        }
        for k, s in shapes.items():
            self.w_in[k] = I(k, s)
        O = lambda n, s: nc.dram_tensor(n, list(s), F32, kind="ExternalOutput").ap()
        self.yp = O("yp", (self.nps, self.sp_len, D)) if self.nps else None
        self.ys = O("ys", (self.nss, self.ss_len, D)) if self.nss else None
        self.scr = {}

        def S(name, kc, ncol):
            t = nc.dram_tensor("scr_" + name, [128, kc, ncol], BF16).ap()
            self.scr[name] = Buf("scr_" + name, t=t, dsem=Sem(nc, "d_scr_" + name))

        for l in range(2):
            for f in (1, 2):
                S(f"g{f}{l}", KC, DFF)
                S(f"u{f}{l}", KC, DFF)
                S(f"d{f}{l}", NJ, D)
            S(f"pg{l}", KC, D)
            S(f"pp{l}", 2, D)
        S("qkv", KC, 1536)
        S("wo", KC, D)
        S("win", KC, 2 * D)
        S("wout", KC, D)
        ntiles = (self.nps * self.sp_len + self.nss * self.ss_len) // T
        self.scr_h_t = nc.dram_tensor("scr_h", [ntiles, 128, KC * T], F32).ap()
        self.scr_h = Buf("scr_h", nslots=ntiles)

    def _sb(self, name, shape, dt, nslots=1, dsem=False):
        t = self.nc.alloc_sbuf_tensor(name, list(shape), dt)
        return Buf(name, t=t, nslots=nslots, dsem=Sem(self.nc, "d_" + name) if dsem else None)

    def _alloc(self):
        nc = self.nc
        sb = self._sb
        self.G = [sb(f"G{i}", (128, KC, T), F32, nslots=2 * KC, dsem=True) for i in range(2)]
        self.hw, self.uT = self.G[0], self.G[1]
        self.next_h = None
        self.hn = sb("hn", (128, KC, T), BF16, nslots=2 * KC)
        self.hid = sb("hid", (128, NJ, T), BF16, nslots=2 * NJ)
        self.vg = sb("vg", (128, D), F32, dsem=True)
        self.sqb = sb("sqb", (128, KC, T), BF16, nslots=2)
        self.sdb = sb("sdb", (128, T), F32, nslots=2)
        self.rsb = sb("rsb", (128, T), F32, nslots=2)
        self.xin = [sb(f"xin{i}", (128, D), F32, dsem=True) for i in range(2)]
        self.yout = [sb(f"yout{i}", (128, D), F32, dsem=True) for i in range(2)]
        self.pin = self.xin[1]
        self.ppw = sb("ppw", (128, 2, 2 * D), BF16, nslots=2, dsem=True)
        self.ppw_loaded = False
        self.pT = sb("pT", (128, 2, T), BF16, nslots=2)
        self.rope = sb("rope", (128, 2, T), F32, dsem=True)
        self.tmpf = [sb(f"tmpf{i}", (128, T), F32) for i in range(6)]
        self.tmpb = [sb(f"tmpb{i}", (128, T), BF16) for i in range(4)]
        self.pexp = [sb(f"pexp{i}", (128, T), BF16) for i in range(4)]
        self.stat = [sb(f"stat{i}", (128, 4), F32) for i in range(2)]
        self.KT = sb("KT", (128, NKV, SMAX), BF16, nslots=NKV * (SMAX // T))
        self.V = sb("V", (128, SMAX // 128, NKV * HD), BF16, nslots=SMAX // 128)
        self.ring = [sb(f"ring{i}", (128, SLOT_ELEMS), BF16, dsem=True) for i in range(RING)]
        self.ring_i = 0
        self.ucache = {}
        self.gains = sb("gains", (128, 8, KC), F32, dsem=True)
        self.gqk = sb("gqk", (128, 2), F32, dsem=True)
        self.ident = sb("ident", (128, 128), F32, dsem=True)
        self.pswap_f = sb("pswap_f", (128, 128), F32, dsem=True)
        self.pswap = sb("pswap", (128, 128), BF16)
        self.onesD = sb("onesD", (128, 128), BF16)
        self.onesD2 = sb("onesD2", (128, 128), BF16)
        self.ones128 = sb("ones128", (128, 128), BF16)
        self.ones1 = sb("ones1", (128, 128), BF16)
        self.onesf = sb("onesf", (1, 128), F32)
        self.wsT = sb("wsT", (128, 8, 128), BF16)
        self.bs2 = sb("bs2", (2, D), BF16, dsem=True)
        self.vnbc = sb("vnbc", (128, D), F32)
        self.psb = [Buf(f"ps{i}", t=nc.alloc_psum_tensor(f"ps{i}", [128, T], F32), psum=True) for i in range(8)]
        self.ps_i = 0
        self.NGEN = 4
        self.pss_i = 0
        self.tf_i = 0
        self.tb_i = 0

    def ps_next(self):
        b = self.psb[self.ps_i % self.NGEN]
        self.ps_i += 1
        return b

    def ps_small(self):
        b = self.psb[5 + self.pss_i % 3]
        self.pss_i += 1
        return b

    def tf(self):
        b = self.tmpf[self.tf_i % len(self.tmpf)]
        self.tf_i += 1
        return b

    def tb(self):
        b = self.tmpb[self.tb_i % len(self.tmpb)]
        self.tb_i += 1
        return b

    def dma(self, E, out, in_, reads, writes, dsem, **kw):
        return self.emit(E, lambda: E.h.dma_start(out=out, in_=in_, **kw), reads, writes, dsem=dsem)

    def act(self, out, in_, func, reads, writes, **kw):
        return self.emit(self.ACT, lambda: self.nc.scalar.activation(out=out, in_=in_, func=func, **kw), reads, writes)

    def mm_group(self, pb, pairs, reads, out_ap=None):
        out_ap = pb.t[:] if out_ap is None else out_ap
        n = len(pairs)

        def fn():
            ins = None
            for i, (l, r) in enumerate(pairs):
                ins = self.nc.tensor.matmul(out_ap, lhsT=l, rhs=r, start=(i == 0), stop=(i == n - 1))
            return ins

        return self.emit(self.PE, fn, reads, [(pb, None)])

    def flush_deferred(self):
        d, self.deferred = self.deferred, []
        for fn in d:
            fn()

    def defer(self, fn, after=2):
        self.deferred.append(fn)
        self.defer_at = self.unit_ctr + after

    def load_unit(self, name, r0, nr, c0, ncol):
        if self.deferred and self.unit_ctr >= self.defer_at:
            self.flush_deferred()
        self.unit_ctr += 1
        slot = self.ring[self.ring_i % RING]
        self.ring_i += 1
        src = self.scr[name]
        n = nr * ncol
        assert n <= SLOT_ELEMS
        out = slot.t[:, 0:n].rearrange("p (a b) -> p a b", a=nr)
        self.dma(self.SP, out, src.t[:, r0:r0 + nr, c0:c0 + ncol], [(src, None)], [(slot, None)], slot.dsem)
        return slot

    def get_unit(self, name, r0, nr, c0, ncol):
        key = (name, r0, nr, c0, ncol)
        ent = self.ucache.get(key)
        if ent is not None and ent[0].gen == ent[1]:
            return ent[0]
        slot = self.load_unit(name, r0, nr, c0, ncol)
        slot.gen += 1
        self.ucache[key] = (slot, slot.gen)
        return slot

    def prepass(self):
        nc = self.nc
        W = self.w_in
        G = self.POOL

        hist = []

        def cast(name, src_ap):
            b = self.scr[name]
            self.dma(G, b.t, src_ap, [], [(b, None)], b.dsem)

        rr = lambda ap: ap.rearrange("(kc p) n -> p kc n", p=128)
        with nc.allow_non_contiguous_dma(reason="one-time tiny parameter loads"):
            gl = [("ffn1_norm", 0), ("ffn1_norm", 1), ("mix_norm", 0), ("mix_norm", 1), ("ffn2_norm", 0),
                  ("ffn2_norm", 1), ("ple_norm", 0), ("ple_norm", 1)]
            for i, (k, l) in enumerate(gl):
                self.dma(self.SP, self.gains.t[:, i, :], W[k][l].rearrange("(kc p) -> p kc", p=128), [],
                         [(self.gains, None)], self.gains.dsem)
            self.dma(self.SP, self.gqk.t[:, 0:1], W["attn_q_norm"][0].rearrange("(p o) -> p o", o=1), [],
                     [(self.gqk, None)], self.gqk.dsem)
            self.dma(self.SP, self.gqk.t[:, 1:2], W["attn_k_norm"][0].rearrange("(p o) -> p o", o=1), [],
                     [(self.gqk, None)], self.gqk.dsem)
        self.dma(self.SP, self.ident.t[:], W["c_ident"], [], [(self.ident, None)], self.ident.dsem)
        self.dma(self.SP, self.pswap_f.t[:], W["c_pswap"], [], [(self.pswap_f, None)], self.pswap_f.dsem)
        self.dma(self.SP, self.vg.t[0:1, :], W["gmlp_v_norm"], [], [(self.vg, None)], self.vg.dsem)
        cast("g10", rr(W["ffn1_w_gate"][0]))
        cast("u10", rr(W["ffn1_w_up"][0]))
        cast("d10", rr(W["ffn1_w_down"][0]))
        cast("qkv", rr(W["attn_w_qkv"][0]))
        cast("pp0", rr(W["ple_w_proj"][0]))
        cast("pp1", rr(W["ple_w_proj"][1]))
        cast("wo", rr(W["attn_w_o"][0]))
        cast("g20", rr(W["ffn2_w_gate"][0]))
        cast("u20", rr(W["ffn2_w_up"][0]))
        cast("d20", rr(W["ffn2_w_down"][0]))
        cast("pg0", rr(W["ple_w_gate"][0]))
        cast("g11", rr(W["ffn1_w_gate"][1]))
        cast("u11", rr(W["ffn1_w_up"][1]))
        cast("d11", rr(W["ffn1_w_down"][1]))
        cast("win", rr(W["gmlp_w_in"][0]))
        cast("wout", rr(W["gmlp_w_out"][0]))
        cast("g21", rr(W["ffn2_w_gate"][1]))
        cast("u21", rr(W["ffn2_w_up"][1]))
        cast("d21", rr(W["ffn2_w_down"][1]))
        cast("pg1", rr(W["ple_w_gate"][1]))
        V = self.DVE
        self.emit(V, lambda: nc.vector.memset(self.onesD.t[:], 1.0 / D), [], [(self.onesD, None)])
        self.emit(V, lambda: nc.vector.memset(self.onesD2.t[:], 1.0 / D), [], [(self.onesD2, None)])
        self.emit(V, lambda: nc.vector.memset(self.ones128.t[:], 1.0 / HD), [], [(self.ones128, None)])
        self.emit(V, lambda: nc.vector.memset(self.ones1.t[:], 1.0), [], [(self.ones1, None)])
        self.emit(V, lambda: nc.vector.memset(self.onesf.t[:], 1.0), [], [(self.onesf, None)])
        self.emit(V, lambda: nc.vector.tensor_copy(out=self.pswap.t[:], in_=self.pswap_f.t[:]),
                  [(self.pswap_f, None)], [(self.pswap, None)])
        for half in range(2):
            pb = self.ps_next()
            self.mm_group(pb, [(self.onesf.t[0:1, :], self.vg.t[0:1, half * T:(half + 1) * T])],
                          [(self.onesf, None), (self.vg, None)])
            self.emit(V, lambda: nc.vector.tensor_copy(out=self.vnbc.t[:, half * T:(half + 1) * T], in_=pb.t[:]),
                      [(pb, None)], [(self.vnbc, None)])
        y0, y1 = self.yout[0], self.yout[1]
        self.dma(self.SP, y0.t[0:1, :], W["gmlp_b_s"].rearrange("o g p -> o (g p)"), [], [(y0, None)], y0.dsem)
        self.emit(V, lambda: nc.vector.tensor_copy(out=self.bs2.t[0:1, :], in_=y0.t[0:1, :]), [(y0, None)], [(self.bs2, None)])
        self.emit(V, lambda: nc.vector.tensor_copy(out=y1.t[0:1, :], in_=self.bs2.t[0:1, :]), [(self.bs2, None)], [(y1, None)])
        self.emit(V, lambda: nc.vector.tensor_tensor(out=y1.t[0:1, :], in0=y0.t[0:1, :], in1=y1.t[0:1, :], op=ALU.subtract),
                  [(y0, None), (y1, None)], [(y1, None)])
        lo16 = self.hid.t[0:1, 0:2, :].rearrange("p a b -> p (a b)")
        self.emit(V, lambda: nc.vector.tensor_copy(out=lo16, in_=y1.t[0:1, :]), [(y1, None)], [(self.hid, [0, 1, 2, 3])])
        self.dma(self.SP, self.bs2.t[1:2, :], lo16, [(self.hid, [0, 1, 2, 3])], [(self.bs2, None)], self.bs2.dsem)
        for g in range(8):
            xb = self.xin[g % 2]
            self.dma(self.SP, xb.t[:, 0:128], W["gmlp_w_s"][0, g], [], [(xb, None)], xb.dsem)
            pb = self.ps_next()
            self.emit(self.PE, lambda: nc.tensor.transpose(out=pb.t[:, 0:128], in_=xb.t[:, 0:128], identity=self.ident.t[:]),
                      [(xb, None), (self.ident, None)], [(pb, None)])
            self.emit(V, lambda: nc.vector.tensor_copy(out=self.wsT.t[:, g, :], in_=pb.t[:, 0:128]),
                      [(pb, None)], [(self.wsT, None)])

    @staticmethod
    def cs(half):
        return slice(half * HW, (half + 1) * HW)

    @staticmethod
    def hs(n, half):
        return [i * 2 + half for i in range(n)]

    def norm_S(self, half):
        c = self.cs(half)
        self.act(self.sqb.t[:, :, c], self.hw.t[:, :, c], AF.Square, [(self.hw, self.hs(KC, half))], [(self.sqb, half)])

    def norm_MN(self, half, gi):
        nc = self.nc
        c = self.cs(half)
        pn = self.ps_next()
        self.mm_group(pn, [((self.onesD if kc % 2 == 0 else self.onesD2).t[:], self.sqb.t[:, kc, c]) for kc in range(KC)],
                      [(self.sqb, half), (self.onesD, None), (self.onesD2, None)], out_ap=pn.t[:, 0:HW])
        self.act(self.sdb.t[:, c], pn.t[:, 0:HW], AF.Ln, [(pn, None)], [(self.sdb, half)], bias=EPS)
        self.act(self.rsb.t[:, c], self.sdb.t[:, c], AF.Exp, [(self.sdb, half)], [(self.rsb, half)], scale=-0.5)
        for kc in range(KC):
            self.emit(self.DVE, lambda: nc.vector.scalar_tensor_tensor(
                out=self.hn.t[:, kc, c], in0=self.hw.t[:, kc, c], scalar=self.gains.t[:, gi, kc:kc + 1], in1=self.rsb.t[:, c],
                op0=ALU.mult, op1=ALU.mult),
                [(self.hw, kc * 2 + half), (self.gains, None), (self.rsb, half)], [(self.hn, kc * 2 + half)])

    def transition(self, tail, nb, gi, head_first=None):
        for b in range(nb - 2):
            tail(0, b)
            tail(1, b)
        tail(0, nb - 2)
        tail(0, nb - 1)
        self.norm_S(0)
        tail(1, nb - 2)
        self.norm_MN(0, gi)
        tail(1, nb - 1)
        self.norm_S(1)
        if head_first is not None:
            head_first()
        self.norm_MN(1, gi)

    def proj_group(self, slot, cw, coff, half=None):
        pb = self.ps_next()
        if half is None:
            pairs = [(slot.t[:, kc * cw + coff: kc * cw + coff + 128], self.hn.t[:, kc, :]) for kc in range(KC)]
            self.mm_group(pb, pairs, [(slot, None), (self.hn, None)])
        else:
            c = self.cs(half)
            pairs = [(slot.t[:, kc * cw + coff: kc * cw + coff + 128], self.hn.t[:, kc, c]) for kc in range(KC)]
            self.mm_group(pb, pairs, [(slot, None), (self.hn, self.hs(KC, half))], out_ap=pb.t[:, 0:HW])
        return pb

    def ffn_gu_block(self, l, which, ub, half):
        nc = self.nc
        f = which + 1
        c0 = ub * 512
        cw = min(512, DFF - c0)
        sg = self.get_unit(f"g{f}{l}", 0, KC, c0, cw)
        su = self.get_unit(f"u{f}{l}", 0, KC, c0, cw)
        c = self.cs(half)
        for jj in range(cw // 128):
            j = ub * 4 + jj
            pg = self.proj_group(sg, cw, jj * 128, half)
            pu = self.proj_group(su, cw, jj * 128, half)
            tg = self.tf()
            self.act(tg.t[:, 0:HW], pg.t[:, 0:HW], AF.Silu, [(pg, None)], [(tg, None)])
            self.emit(self.DVE, lambda: nc.vector.tensor_tensor(out=self.hid.t[:, j, c], in0=pu.t[:, 0:HW], in1=tg.t[:, 0:HW],
                                                                op=ALU.mult),
                      [(pu, None), (tg, None)], [(self.hid, 2 * j + half)])

    def ffn_down_block(self, l, which, half, b):
        nc = self.nc
        dname = f"d{which + 1}{l}"
        c = self.cs(half)
        mp, mms = [(0, (0, 1)), (1, (0, 1)), (2, (0, 1)), (3, (0,)), (3, (1,))][b]
        sd = [self.get_unit(dname, jh * 11, 11, mp * 256, 256) for jh in range(2)]
        for mm in mms:
            m = mp * 2 + mm
            po = self.ps_next()
            pairs = [(sd[j // 11].t[:, (j % 11) * 256 + mm * 128:(j % 11) * 256 + mm * 128 + 128], self.hid.t[:, j, c])
                     for j in range(NJ)]
            self.mm_group(po, pairs, [(sd[0], None), (sd[1], None), (self.hid, self.hs(NJ, half))], out_ap=po.t[:, 0:HW])
            self.emit(self.DVE, lambda: nc.vector.scalar_tensor_tensor(
                out=self.hw.t[:, m, c], in0=po.t[:, 0:HW], scalar=0.5, in1=self.hw.t[:, m, c], op0=ALU.mult, op1=ALU.add),
                [(po, None), (self.hw, m * 2 + half)], [(self.hw, m * 2 + half)])

    def ffn_body(self, l, which, first_done=True):
        order = [(0, 0), (1, 0), (0, 1), (1, 1)] + [(ub, h) for ub in range(2, 6) for h in range(2)]
        if first_done:
            order = order[1:]
        for ub, h in order:
            self.ffn_gu_block(l, which, ub, h)

    def ffn_head(self, l, which):
        return lambda: self.ffn_gu_block(l, which, 0, 0)

    def ffn_tail(self, l, which):
        return (lambda half, b: self.ffn_down_block(l, which, half, b)), 5

    def qk_post_stages(self, n, get_pq, gcol, rp, out_ap, out_dep):
        nc = self.nc
        st = {}

        def s1(i):
            pq = get_pq(i)
            sq = self.tb()
            self.act(sq.t[:], pq.t[:], AF.Square, [(pq, None)], [(sq, None)])
            pn = self.ps_small()
            self.mm_group(pn, [(self.ones128.t[:], sq.t[:])], [(sq, None), (self.ones128, None)])
            st[i] = dict(pq=pq, pn=pn)

        def s2(i):
            d = st[i]
            sd = self.tf()
            self.act(sd.t[:], d["pn"].t[:], AF.Ln, [(d["pn"], None)], [(sd, None)], bias=EPS)
            rs = self.tf()
            self.act(rs.t[:], sd.t[:], AF.Exp, [(sd, None)], [(rs, None)], scale=-0.5)
            kn = self.tb()
            self.emit(self.DVE, lambda: nc.vector.scalar_tensor_tensor(
                out=kn.t[:], in0=d["pq"].t[:], scalar=self.gqk.t[:, gcol:gcol + 1], in1=rs.t[:], op0=ALU.mult, op1=ALU.mult),
                [(d["pq"], None), (self.gqk, None), (rs, None)], [(kn, None)])
            pr = self.ps_small()
            self.mm_group(pr, [(self.pswap.t[:], kn.t[:])], [(self.pswap, None), (kn, None)])
            d["kn"] = kn
            d["pr"] = pr

        def s3(i):
            d = st.pop(i)
            t1 = self.tf()
            self.emit(self.DVE, lambda: nc.vector.tensor_tensor(out=t1.t[:], in0=d["kn"].t[:], in1=rp.t[:, 0, :], op=ALU.mult),
                      [(d["kn"], None), (rp, None)], [(t1, None)])
            t2 = self.tf()
            self.emit(self.DVE, lambda: nc.vector.tensor_tensor(out=t2.t[:], in0=d["pr"].t[:], in1=rp.t[:, 1, :], op=ALU.mult),
                      [(d["pr"], None), (rp, None)], [(t2, None)])
            ob, osl = out_dep(i)
            self.emit(self.DVE, lambda: nc.vector.tensor_tensor(out=out_ap(i), in0=t1.t[:], in1=t2.t[:], op=ALU.add),
                      [(t1, None), (t2, None)], [(ob, osl)])

        return s1, s2, s3

    @staticmethod
    def skew(n, stages):
        for step in range(n + len(stages) - 1):
            for si, f in enumerate(stages):
                i = step - si
                if 0 <= i < n:
                    f(i)

    def load_rope(self, tile_in_seq):
        rp = self.rope
        W = self.w_in
        t0 = tile_in_seq * T
        self.dma(self.SP, rp.t[:, 0, :], W["c_cos"][:, t0:t0 + T], [], [(rp, None)], rp.dsem)
        self.dma(self.SP, rp.t[:, 1, :], W["c_sin"][:, t0:t0 + T], [], [(rp, None)], rp.dsem)
        return rp

    def x_chunk(self, xseq, ti, c):
        nc = self.nc
        xb = self.xin[c % 2]
        r0 = ti * T + c * 128
        half = c // 2
        self.dma(self.SP, xb.t[:], xseq[r0:r0 + 128, :], [], [(xb, None)], xb.dsem)
        for q4 in range(2):
            pb = self.ps_next()

            def fn():
                ins = None
                for q in range(4):
                    kc = q4 * 4 + q
                    ins = nc.tensor.transpose(out=pb.t[:, q * 128:(q + 1) * 128], in_=xb.t[:, kc * 128:(kc + 1) * 128],
                                              identity=self.ident.t[:])
                return ins

            self.emit(self.PE, fn, [(xb, None), (self.ident, None)], [(pb, None)])
            dst = self.hw.t[:, q4 * 4:q4 * 4 + 4, c * 128:(c + 1) * 128]
            src = pb.t[:].rearrange("p (a b) -> p a b", a=4)
            wr = [(self.hw, [(q4 * 4 + q) * 2 + half for q in range(4)])]
            if (c + q4) % 2 == 0:
                self.emit(self.DVE, lambda: nc.vector.tensor_copy(out=dst, in_=src), [(pb, None)], wr)
            else:
                self.act(dst, src, AF.Copy, [(pb, None)], wr)

    def tile_A(self, xseq, ti, gt):
        nc = self.nc
        rp = self.load_rope(ti)
        self.transition(lambda half, b: self.x_chunk(xseq, ti, half * 2 + b), 2, 0, self.ffn_head(0, 0))
        self.ffn_body(0, 0)
        tail, nb = self.ffn_tail(0, 0)
        self.transition(tail, nb, 2, None)
        dst = self.scr_h_t[gt].rearrange("p (a b) -> p a b", a=KC)
        self.dma(self.SP, dst, self.hw.t[:], [(self.hw, None)], [(self.scr_h, gt)], self.hw.dsem)
        sk = self.get_unit("qkv", 0, KC, 1024, 256)
        sv = self.get_unit("qkv", 0, KC, 1280, 256)
        pqs = {}

        def s0(i):
            pqs[i] = self.proj_group(sk, 256, i * 128)

        s1, s2, s3 = self.qk_post_stages(
            NKV, lambda i: pqs[i], 1, rp,
            lambda i: self.KT.t[:, i, ti * T:(ti + 1) * T], lambda i: (self.KT, i * (SMAX // T) + ti))
        self.skew(NKV, [s0, s1, s2, s3])
        for c in range(4):
            pb = self.ps_next()
            pairs = [(self.hn.t[:, kc, c * 128:(c + 1) * 128], sv.t[:, kc * 256:(kc + 1) * 256]) for kc in range(KC)]
            self.mm_group(pb, pairs, [(sv, None), (self.hn, self.hs(KC, c // 2))], out_ap=pb.t[:, 0:256])
            ch = ti * 4 + c
            self.act(self.V.t[:, ch, :], pb.t[:, 0:256], AF.Copy, [(pb, None)], [(self.V, ch)])

    def attention_core(self, S, ti, rp):
        nc = self.nc
        squ = [self.get_unit("qkv", 0, KC, ub * 512, 512) for ub in range(2)]
        pqs = {}

        def s0(h):
            pqs[h] = self.proj_group(squ[h // 4], 512, (h % 4) * 128)

        s1, s2, s3 = self.qk_post_stages(NQH, lambda h: pqs.pop(h), 0, rp, lambda h: self.hid.t[:, h, :],
                                         lambda h: (self.hid, [2 * h, 2 * h + 1]))
        self.skew(NQH, [s0, s1, s2, s3])
        nkc = S // 128
        scale = float(HD) ** -0.5
        for h in range(NQH):
            kvh = h // 4
            Ob, Db = self.psb[4 + 2 * (h % 2)], self.psb[5 + 2 * (h % 2)]
            st = {}

            def f0(kc):
                ps = self.ps_next()
                self.mm_group(ps, [(self.KT.t[:, kvh, kc * 128:(kc + 1) * 128], self.hid.t[:, h, :])],
                              [(self.KT, kvh * (SMAX // T) + kc // 4), (self.hid, [2 * h, 2 * h + 1])])
                pe = self.pexp[kc % len(self.pexp)]
                self.act(pe.t[:], ps.t[:], AF.Exp, [(ps, None)], [(pe, None)], scale=scale)
                st[kc] = pe

            def f1(kc):
                pe = st.pop(kc)
                self.emit(self.PE, lambda: nc.tensor.matmul(Ob.t[:], lhsT=self.V.t[:, kc, kvh * HD:(kvh + 1) * HD], rhs=pe.t[:],
                                                            start=(kc == 0), stop=(kc == nkc - 1)),
                          [(self.V, kc), (pe, None)], [(Ob, None)])
                self.emit(self.PE, lambda: nc.tensor.matmul(Db.t[:], lhsT=self.ones1.t[:], rhs=pe.t[:],
                                                            start=(kc == 0), stop=(kc == nkc - 1)),
                          [(self.ones1, None), (pe, None)], [(Db, None)])

            self.skew(nkc, [f0, lambda kc: None, f1])
            rd = self.tf()
            self.emit(self.DVE, lambda: nc.vector.reciprocal(out=rd.t[:], in_=Db.t[:]), [(Db, None)], [(rd, None)])
            self.emit(self.DVE, lambda: nc.vector.tensor_tensor(out=self.hid.t[:, 8 + h, :], in0=Ob.t[:], in1=rd.t[:], op=ALU.mult),
                      [(Ob, None), (rd, None)], [(self.hid, [2 * (8 + h), 2 * (8 + h) + 1])])

    def oproj_block(self, half, ub):
        nc = self.nc
        c = self.cs(half)
        so = self.get_unit("wo", 0, KC, ub * 512, 512)
        for mm in range(4):
            m = ub * 4 + mm
            po = self.ps_next()
            pairs = [(so.t[:, h * 512 + mm * 128:h * 512 + mm * 128 + 128], self.hid.t[:, 8 + h, c]) for h in range(NQH)]
            self.mm_group(po, pairs, [(so, None), (self.hid, [2 * (8 + h) + half for h in range(NQH)])], out_ap=po.t[:, 0:HW])
            self.emit(self.DVE, lambda: nc.vector.tensor_tensor(out=self.hw.t[:, m, c], in0=po.t[:, 0:HW], in1=self.hw.t[:, m, c],
                                                                op=ALU.add),
                      [(po, None), (self.hw, m * 2 + half)], [(self.hw, m * 2 + half)])

    def ple_prep(self, l, pseq_l, ti):
        nc = self.nc
        pb_in = self.pin
        pin_v = pb_in.t[:].rearrange("p (c d) -> p c d", c=4)
        self.dma(self.SP, pin_v, pseq_l[ti * T:(ti + 1) * T, :].rearrange("(c p) d -> p c d", p=128), [],
                 [(pb_in, None)], pb_in.dsem)
        if not self.ppw_loaded:
            self.ppw_loaded = True
            for ll in range(2):
                src = self.scr[f"pp{ll}"]
                self.dma(self.SP, self.ppw.t[:, ll, :].rearrange("p (a b) -> p a b", a=2), src.t, [(src, None)],
                         [(self.ppw, ll)], self.ppw.dsem)
        for k2 in range(2):
            pb = self.ps_next()

            def fn():
                ins = None
                for c in range(4):
                    ins = nc.tensor.transpose(out=pb.t[:, c * 128:(c + 1) * 128], in_=pin_v[:, c, k2 * 128:(k2 + 1) * 128],
                                              identity=self.ident.t[:])
                return ins

            self.emit(self.PE, fn, [(pb_in, None), (self.ident, None)], [(pb, None)])
            self.act(self.pT.t[:, k2, :], pb.t[:], AF.Copy, [(pb, None)], [(self.pT, k2)])

    def ple_block(self, l, half, ub):
        nc = self.nc
        c = self.cs(half)
        sg = self.get_unit(f"pg{l}", 0, KC, ub * 512, 512)
        for mm in range(4):
            m = ub * 4 + mm
            pg = self.proj_group(sg, 512, mm * 128, half)
            tg = self.tf()
            self.act(tg.t[:, 0:HW], pg.t[:, 0:HW], AF.Sigmoid, [(pg, None)], [(tg, None)])
            pp = self.ps_next()
            pairs = [(self.ppw.t[:, l, k2 * D + m * 128:k2 * D + m * 128 + 128], self.pT.t[:, k2, c]) for k2 in range(2)]
            self.mm_group(pp, pairs, [(self.ppw, l), (self.pT, None)], out_ap=pp.t[:, 0:HW])
            t2 = self.tf()
            self.emit(self.DVE, lambda: nc.vector.tensor_tensor(out=t2.t[:, 0:HW], in0=pp.t[:, 0:HW], in1=tg.t[:, 0:HW], op=ALU.mult),
                      [(pp, None), (tg, None)], [(t2, None)])
            self.emit(self.POOL, lambda: nc.gpsimd.tensor_tensor(out=self.hw.t[:, m, c], in0=t2.t[:, 0:HW], in1=self.hw.t[:, m, c],
                                                                 op=ALU.add),
                      [(t2, None), (self.hw, m * 2 + half)], [(self.hw, m * 2 + half)])

    def gmlp_u_block(self, ub, half):
        su = self.get_unit("win", 0, KC, ub * 512, 512)
        c = self.cs(half)
        for mm in range(4):
            m = ub * 4 + mm
            pu = self.proj_group(su, 512, mm * 128, half)
            self.act(self.uT.t[:, m, c], pu.t[:, 0:HW], AF.Gelu, [(pu, None)], [(self.uT, m * 2 + half)])

    def gmlp_v_chunk(self, c):
        nc = self.nc
        half = c // 2
        svv = [self.get_unit("win", 0, KC, D + hf * 512, 512) for hf in range(2)]
        stt = self.stat[c % 2]
        pbs = []
        for hf in range(2):
            pb = self.ps_next()
            pairs = [(self.hn.t[:, kc, c * 128:(c + 1) * 128], svv[hf].t[:, kc * 512:(kc + 1) * 512]) for kc in range(KC)]
            self.mm_group(pb, pairs, [(svv[hf], None), (self.hn, self.hs(KC, half))])
            pbs.append(pb)
        for hf in range(2):
            self.act(self.vg.t[:, hf * 512:(hf + 1) * 512], pbs[hf].t[:], AF.Gelu, [(pbs[hf], None)], [(self.vg, None)])
        for hf in range(2):
            junk = self.tf()
            self.act(junk.t[:], self.vg.t[:, hf * 512:(hf + 1) * 512], AF.Square, [(self.vg, None)],
                     [(junk, None), (stt, None)], accum_out=stt.t[:, hf:hf + 1])
        self.emit(self.DVE, lambda: nc.vector.tensor_tensor(out=stt.t[:, 2:3], in0=stt.t[:, 0:1], in1=stt.t[:, 1:2], op=ALU.add),
                  [(stt, None)], [(stt, None)])
        self.act(stt.t[:, 3:4], stt.t[:, 2:3], AF.Ln, [(stt, None)], [(stt, None)], bias=EPS, scale=1.0 / D)
        self.act(stt.t[:, 2:3], stt.t[:, 3:4], AF.Exp, [(stt, None)], [(stt, None)], scale=-0.5)
        vdst = self.hid.t[:, 2 * c:2 * c + 2, :].rearrange("p a b -> p (a b)")
        self.emit(self.DVE, lambda: nc.vector.scalar_tensor_tensor(
            out=vdst, in0=self.vg.t[:], scalar=stt.t[:, 2:3], in1=self.vnbc.t[:], op0=ALU.mult, op1=ALU.mult),
            [(self.vg, None), (stt, None), (self.vnbc, None)], [(self.hid, [4 * c, 4 * c + 1, 4 * c + 2, 4 * c + 3])])

    def gmlp_spatial(self, g, half):
        nc = self.nc
        c_ = self.cs(half)
        pb = self.ps_next()

        def fn():
            ins = None
            for cc in range(2):
                c = half * 2 + cc
                vsl = self.hid.t[:, 2 * c + g // 4, (g % 4) * 128:(g % 4) * 128 + 128]
                nc.tensor.matmul(pb.t[:, cc * 128:(cc + 1) * 128], lhsT=vsl, rhs=self.wsT.t[:, g, :], start=True, stop=False)
                ins = nc.tensor.matmul(pb.t[:, cc * 128:(cc + 1) * 128], lhsT=self.ones1.t[0:2, :],
                                       rhs=self.bs2.t[0:2, g * 128:(g + 1) * 128], start=False, stop=True)
            return ins

        rd = [(self.hid, [2 * (2 * (half * 2 + cc) + g // 4) + hh for cc in range(2) for hh in range(2)]),
              (self.wsT, None), (self.ones1, None), (self.bs2, None)]
        self.emit(self.PE, fn, rd, [(pb, None)])
        self.emit(self.DVE, lambda: nc.vector.tensor_tensor(out=self.hid.t[:, 8 + g, c_], in0=pb.t[:, 0:HW], in1=self.uT.t[:, g, c_],
                                                            op=ALU.mult),
                  [(pb, None), (self.uT, g * 2 + half)], [(self.hid, 2 * (8 + g) + half)])

    def gmlp_out_block(self, half, ub):
        nc = self.nc
        c = self.cs(half)
        so = self.get_unit("wout", 0, KC, ub * 512, 512)
        for mm in range(4):
            m = ub * 4 + mm
            po = self.ps_next()
            pairs = [(so.t[:, kc * 512 + mm * 128:kc * 512 + mm * 128 + 128], self.hid.t[:, 8 + kc, c]) for kc in range(KC)]
            self.mm_group(po, pairs, [(so, None), (self.hid, [2 * (8 + kc) + half for kc in range(KC)])], out_ap=po.t[:, 0:HW])
            self.emit(self.DVE, lambda: nc.vector.tensor_tensor(out=self.hw.t[:, m, c], in0=po.t[:, 0:HW], in1=self.hw.t[:, m, c],
                                                                op=ALU.add),
                      [(po, None), (self.hw, m * 2 + half)], [(self.hw, m * 2 + half)])

    def out_chunk(self, yseq, ti, c):
        nc = self.nc
        half = c // 2
        yb = self.yout[c % 2]
        for q4 in range(2):
            pb = self.ps_next()

            def fn():
                ins = None
                for q in range(4):
                    kc = q4 * 4 + q
                    ins = nc.tensor.transpose(out=pb.t[:, q * 128:(q + 1) * 128], in_=self.hw.t[:, kc, c * 128:(c + 1) * 128],
                                              identity=self.ident.t[:])
                return ins

            self.emit(self.PE, fn, [(self.hw, [(q4 * 4 + q) * 2 + half for q in range(4)]), (self.ident, None)], [(pb, None)])
            if q4 == 0:
                self.emit(self.DVE, lambda: nc.vector.tensor_copy(out=yb.t[:, 0:512], in_=pb.t[:]), [(pb, None)], [(yb, None)])
            else:
                self.act(yb.t[:, 512:1024], pb.t[:], AF.Copy, [(pb, None)], [(yb, None)])
        r0 = ti * T + c * 128
        self.dma(self.SP, yseq[r0:r0 + 128, :], yb.t[:], [(yb, None)], [], yb.dsem)

    def prefetch_next_h(self):
        if self.next_h is None:
            return
        gt = self.next_h
        src = self.scr_h_t[gt].rearrange("p (a b) -> p a b", a=KC)
        self.dma(self.SP, self.uT.t[:], src, [(self.scr_h, gt)], [(self.uT, None)], self.uT.dsem)
        self.prefetched = gt

    def tile_B(self, S, yseq, pseq, ti, gt, stop_after=99, has_next=False):
        rp = self.load_rope(ti)
        if getattr(self, "prefetched", None) == gt:
            self.hw, self.uT = self.uT, self.hw
        else:
            src = self.scr_h_t[gt].rearrange("p (a b) -> p a b", a=KC)
            self.dma(self.SP, self.hw.t[:], src, [(self.scr_h, gt)], [(self.hw, None)], self.hw.dsem)
        self.next_h = gt + 1 if has_next else None
        self.norm_S(0)
        self.norm_S(1)
        self.norm_MN(0, 2)
        self.norm_MN(1, 2)
        self.attention_core(S, ti, rp)
        self.transition(lambda half, b: self.oproj_block(half, b), 2, 4, self.ffn_head(0, 1))
        self.ffn_body(0, 1)
        tail, nb = self.ffn_tail(0, 1)
        self.ple_prep(0, pseq[0], ti)
        self.transition(tail, nb, 6, lambda: self.ple_block(0, 0, 0))
        self.ple_block(0, 0, 1)
        self.norm_S(0)
        self.ple_block(0, 1, 0)
        self.norm_MN(0, 1)
        self.ple_block(0, 1, 1)
        self.norm_S(1)
        self.ffn_gu_block(1, 0, 0, 0)
        self.norm_MN(1, 1)
        self.ffn_body(1, 0)
        tail, nb = self.ffn_tail(1, 0)
        self.transition(tail, nb, 3, lambda: self.gmlp_u_block(0, 0))
        self.gmlp_v_chunk(0)
        self.gmlp_v_chunk(1)
        self.gmlp_u_block(1, 0)
        self.gmlp_v_chunk(2)
        self.gmlp_v_chunk(3)
        self.gmlp_u_block(0, 1)
        self.gmlp_u_block(1, 1)
        for ub in range(2):
            self.get_unit("wout", 0, KC, ub * 512, 512)
        for g in range(8):
            self.gmlp_spatial(g, 0)
        for g in range(8):
            self.gmlp_spatial(g, 1)
        self.prefetch_next_h()
        self.transition(lambda half, b: self.gmlp_out_block(half, b), 2, 5, self.ffn_head(1, 1))
        self.ffn_body(1, 1)
        tail, nb = self.ffn_tail(1, 1)
        self.ple_prep(1, pseq[1], ti)
        self.transition(tail, nb, 7, lambda: self.ple_block(1, 0, 0))
        self.ple_block(1, 0, 1)
        self.out_chunk(yseq, ti, 0)
        self.ple_block(1, 1, 0)
        self.out_chunk(yseq, ti, 1)
        self.ple_block(1, 1, 1)
        self.out_chunk(yseq, ti, 2)
        self.out_chunk(yseq, ti, 3)

    def build(self, stop_after=99):
        self.prepass()
        seqs = []
        for b in range(self.nps):
            seqs.append((self.sp_len, self.xp[b], self.yp[b], [self.pp[0, b], self.pp[1, b]]))
        for b in range(self.nss):
            seqs.append((self.ss_len, self.xs[b], self.ys[b], [self.psm[0, b], self.psm[1, b]]))
        gt0 = 0
        for (S, xseq, yseq, pseq) in seqs:
            nt = S // T
            for ti in range(nt):
                self.tile_A(xseq, ti, gt0 + ti)
            for ti in range(nt):
                self.tile_B(S, yseq, pseq, ti, gt0 + ti, stop_after=stop_after, has_next=(ti + 1 < nt))
            gt0 += nt
        self.flush_deferred()
        for yb in self.yout:
            if yb.dsem.count:
                self.SP.h.wait_ge(yb.dsem.h, yb.dsem.count)
        return self.nc


def _consts():
    half = HD // 2
    inv = (10000.0 ** (-np.arange(0, half, 2, dtype=np.float32) / np.float32(half))).astype(np.float32)
    t = np.arange(SMAX)
    row = (t // 64).astype(np.float32)
    col = (t % 64).astype(np.float32)
    ang_r = row[:, None] * inv[None, :]
    ang_c = col[:, None] * inv[None, :]
    cos = np.zeros((128, SMAX), np.float32)
    sin = np.zeros((128, SMAX), np.float32)
    for d in range(128):
        a = (ang_r if d < 64 else ang_c)[:, d % 32]
        cos[d] = np.cos(a)
        sgn = -1.0 if (d % 64) < 32 else 1.0
        sin[d] = sgn * np.sin(a)
    ident = np.eye(128, dtype=np.float32)
    ps = np.zeros((128, 128), np.float32)
    for d in range(128):
        partner = d + 32 if (d % 64) < 32 else d - 32
        ps[partner, d] = 1.0
    return dict(c_cos=cos, c_sin=sin, c_ident=ident, c_pswap=ps)


_WKEYS = ["ffn1_norm", "ffn1_w_gate", "ffn1_w_up", "ffn1_w_down", "mix_norm", "attn_w_qkv", "attn_q_norm", "attn_k_norm",
          "attn_w_o", "gmlp_w_in", "gmlp_v_norm", "gmlp_w_s", "gmlp_b_s", "gmlp_w_out", "ffn2_norm", "ffn2_w_gate",
          "ffn2_w_up", "ffn2_w_down", "ple_norm", "ple_w_gate", "ple_w_proj"]


def kernel(**inputs):
    f = lambda a: np.ascontiguousarray(np.asarray(a, dtype=np.float32))
    xp, xs = f(inputs["x_prompt"]), f(inputs["x_sample"])
    pp, ps = f(inputs["p_prompt"]), f(inputs["p_sample"])
    nps, nss = xp.shape[0] // NCORES, xs.shape[0] // NCORES
    nc = Builder(nps, nss).build()
    shared = {k: f(inputs[k]) for k in _WKEYS}
    shared.update(_consts())
    in_maps = []
    for c in range(NCORES):
        m = dict(shared)
        m["xp"] = xp[c * nps:(c + 1) * nps]
        m["xs"] = xs[c * nss:(c + 1) * nss]
        m["pp"] = np.ascontiguousarray(pp[:, c * nps:(c + 1) * nps])
        m["psm"] = np.ascontiguousarray(ps[:, c * nss:(c + 1) * nss])
        in_maps.append(m)
    res = run_bass_kernel_spmd(nc, in_maps, core_ids=list(range(NCORES)))
    yp = np.concatenate([r["yp"] for r in res.results], axis=0).astype(np.float32)
    ys = np.concatenate([r["ys"] for r in res.results], axis=0).astype(np.float32)
    return (yp, ys)
```

```python
import numpy as np
import concourse.bass as bass
import concourse.mybir as mybir
from concourse.bass_utils import run_bass_kernel_spmd

F32 = mybir.dt.float32
BF16 = mybir.dt.bfloat16
AF = mybir.ActivationFunctionType
ALU = mybir.AluOpType

D = 1024
DFF = 2816
NJ = 22
KC = 8
T = 512
HD = 128
NQH = 8
NKV = 2
PLE = 256
EPS = 1e-6
NCORES = 8
SP_LEN = 2048
SS_LEN = 4096
SMAX = 4096
RING = 4
SLOT_ELEMS = 4096
HW = 256


class Sem:
    def __init__(self, nc, name, owner=None):
        self.h = nc.alloc_semaphore(name)
        self.owner = owner
        self.count = 0


class Eng:
    def __init__(self, nc, name, handle, is_pe=False):
        self.name = name
        self.h = handle
        self.is_pe = is_pe
        self.sem = Sem(nc, "s_" + name, owner=self)
        self.seen = {}


class Buf:
    def __init__(self, name, t=None, nslots=1, psum=False, dsem=None):
        self.name = name
        self.t = t
        self.n = nslots
        self.psum = psum
        self.dsem = dsem
        self.gen = 0
        self.w = [None] * nslots
        self.r = [dict() for _ in range(nslots)]

    def slots(self, sl):
        if sl is None:
            return range(self.n)
        if isinstance(sl, int):
            return (sl,)
        return sl


class Builder:
    def __init__(self, nps, nss, sp_len=SP_LEN, ss_len=SS_LEN):
        self.nps, self.nss, self.sp_len, self.ss_len = nps, nss, sp_len, ss_len
        nc = bass.Bass("TRN2", target_bir_lowering=False)
        self.nc = nc
        self.PE = Eng(nc, "pe", nc.tensor, is_pe=True)
        self.ACT = Eng(nc, "act", nc.scalar)
        self.DVE = Eng(nc, "dve", nc.vector)
        self.POOL = Eng(nc, "pool", nc.gpsimd)
        self.SP = Eng(nc, "sp", nc.sync)
        self.deferred = []
        self.unit_ctr = 0
        self.defer_at = 0
        self._declare_dram()
        self._alloc()

    def emit(self, E, fn, reads=(), writes=(), dsem=None):
        raw, oth = {}, {}

        def add(d, dep):
            if dep is None:
                return
            s, v = dep
            if d.get(s, 0) < v:
                d[s] = v

        for b, sl in reads:
            for i in b.slots(sl):
                add(raw, b.w[i])
                if b.psum:
                    for dep in b.r[i].items():
                        add(oth, dep)
        for b, sl in writes:
            for i in b.slots(sl):
                add(oth, b.w[i])
                for dep in b.r[i].items():
                    add(oth, dep)
        need = dict(oth)
        for s, v in raw.items():
            if need.get(s, 0) < v:
                need[s] = v
        for s, v in need.items():
            if s.owner is E and E.is_pe:
                continue
            if E.seen.get(s, 0) >= v:
                continue
            E.h.wait_ge(s.h, v)
            E.seen[s] = v
        ins = fn()
        if dsem is not None:
            dsem.count += 16
            ins.then_inc(dsem.h, 16)
            dep = (dsem, dsem.count)
        else:
            E.sem.count += 1
            ins.then_inc(E.sem.h, 1)
            dep = (E.sem, E.sem.count)
        for b, sl in reads:
            for i in b.slots(sl):
                if b.r[i].get(dep[0], 0) < dep[1]:
                    b.r[i][dep[0]] = dep[1]
        for b, sl in writes:
            for i in b.slots(sl):
                b.w[i] = dep
                b.r[i] = {}
        return dep

    def _declare_dram(self):
        nc = self.nc
        I = lambda n, s: nc.dram_tensor(n, list(s), F32, kind="ExternalInput").ap()
        self.xp = I("xp", (self.nps, self.sp_len, D)) if self.nps else None
        self.xs = I("xs", (self.nss, self.ss_len, D)) if self.nss else None
        self.pp = I("pp", (2, self.nps, self.sp_len, PLE)) if self.nps else None
        self.psm = I("psm", (2, self.nss, self.ss_len, PLE)) if self.nss else None
        self.w_in = {}
        shapes = {
            "ffn1_norm": (2, D), "ffn1_w_gate": (2, D, DFF), "ffn1_w_up": (2, D, DFF), "ffn1_w_down": (2, DFF, D),
            "mix_norm": (2, D), "attn_w_qkv": (1, D, 1536), "attn_q_norm": (1, HD), "attn_k_norm": (1, HD),
            "attn_w_o": (1, D, D), "gmlp_w_in": (1, D, 2 * D), "gmlp_v_norm": (1, D), "gmlp_w_s": (1, 8, 128, 128),
            "gmlp_b_s": (1, 8, 128), "gmlp_w_out": (1, D, D), "ffn2_norm": (2, D), "ffn2_w_gate": (2, D, DFF),
            "ffn2_w_up": (2, D, DFF), "ffn2_w_down": (2, DFF, D), "ple_norm": (2, D), "ple_w_gate": (2, D, D),
            "ple_w_proj": (2, PLE, D),
            "c_cos": (128, SMAX), "c_sin": (128, SMAX), "c_ident": (128, 128), "c_pswap": (128, 128),
        }
        for k, s in shapes.items():
            self.w_in[k] = I(k, s)
        O = lambda n, s: nc.dram_tensor(n, list(s), F32, kind="ExternalOutput").ap()
        self.yp = O("yp", (self.nps, self.sp_len, D)) if self.nps else None
        self.ys = O("ys", (self.nss, self.ss_len, D)) if self.nss else None
        self.scr = {}

        def S(name, kc, ncol):
            t = nc.dram_tensor("scr_" + name, [128, kc, ncol], BF16).ap()
            self.scr[name] = Buf("scr_" + name, t=t, dsem=Sem(nc, "d_scr_" + name))

        for l in range(2):
            for f in (1, 2):
                S(f"g{f}{l}", KC, DFF)
                S(f"u{f}{l}", KC, DFF)
                S(f"d{f}{l}", NJ, D)
            S(f"pg{l}", KC, D)
            S(f"pp{l}", 2, D)
        S("qkv", KC, 1536)
        S("wo", KC, D)
        S("win", KC, 2 * D)
        S("wout", KC, D)
        ntiles = (self.nps * self.sp_len + self.nss * self.ss_len) // T
        self.scr_h_t = nc.dram_tensor("scr_h", [ntiles, 128, KC * T], F32).ap()
        self.scr_h = Buf("scr_h", nslots=ntiles)

    def _sb(self, name, shape, dt, nslots=1, dsem=False):
        t = self.nc.alloc_sbuf_tensor(name, list(shape), dt)
        return Buf(name, t=t, nslots=nslots, dsem=Sem(self.nc, "d_" + name) if dsem else None)

    def _alloc(self):
        nc = self.nc
        sb = self._sb
        self.G = [sb(f"G{i}", (128, KC, T), F32, nslots=2 * KC, dsem=True) for i in range(2)]
        self.hw, self.uT = self.G[0], self.G[1]
        self.next_h = None
        self.hn = sb("hn", (128, KC, T), BF16, nslots=2 * KC)
        self.hid = sb("hid", (128, NJ, T), BF16, nslots=2 * NJ)
        self.vg = sb("vg", (128, D), F32, dsem=True)
        self.sqb = sb("sqb", (128, KC, T), BF16, nslots=2)
        self.sdb = sb("sdb", (128, T), F32, nslots=2)
        self.rsb = sb("rsb", (128, T), F32, nslots=2)
        self.xin = [sb(f"xin{i}", (128, D), F32, dsem=True) for i in range(2)]
        self.yout = [sb(f"yout{i}", (128, D), F32, dsem=True) for i in range(2)]
        self.pin = self.xin[1]
        self.ppw = sb("ppw", (128, 2, 2 * D), BF16, nslots=2, dsem=True)
        self.ppw_loaded = False
        self.pT = sb("pT", (128, 2, T), BF16, nslots=2)
        self.rope = sb("rope", (128, 2, T), F32, dsem=True)
        self.tmpf = [sb(f"tmpf{i}", (128, T), F32) for i in range(6)]
        self.tmpb = [sb(f"tmpb{i}", (128, T), BF16) for i in range(4)]
        self.pexp = [sb(f"pexp{i}", (128, T), BF16) for i in range(4)]
        self.stat = [sb(f"stat{i}", (128, 4), F32) for i in range(2)]
        self.KT = sb("KT", (128, NKV, SMAX), BF16, nslots=NKV * (SMAX // T))
        self.V = sb("V", (128, SMAX // 128, NKV * HD), BF16, nslots=SMAX // 128)
        self.ring = [sb(f"ring{i}", (128, SLOT_ELEMS), BF16, dsem=True) for i in range(RING)]
        self.ring_i = 0
        self.ucache = {}
        self.gains = sb("gains", (128, 8, KC), F32, dsem=True)
        self.gqk = sb("gqk", (128, 2), F32, dsem=True)
        self.ident = sb("ident", (128, 128), F32, dsem=True)
        self.pswap_f = sb("pswap_f", (128, 128), F32, dsem=True)
        self.pswap = sb("pswap", (128, 128), BF16)
        self.onesD = sb("onesD", (128, 128), BF16)
        self.ones128 = sb("ones128", (128, 128), BF16)
        self.ones1 = sb("ones1", (128, 128), BF16)
        self.onesf = sb("onesf", (1, 128), F32)
        self.wsT = sb("wsT", (128, 8, 128), BF16)
        self.bs2 = sb("bs2", (2, D), BF16, dsem=True)
        self.vnbc = sb("vnbc", (128, D), F32)
        self.psb = [Buf(f"ps{i}", t=nc.alloc_psum_tensor(f"ps{i}", [128, T], F32), psum=True) for i in range(8)]
        self.ps_i = 0
        self.NGEN = 4
        self.pss_i = 0
        self.tf_i = 0
        self.tb_i = 0

    def ps_next(self):
        b = self.psb[self.ps_i % self.NGEN]
        self.ps_i += 1
        return b

    def ps_small(self):
        b = self.psb[5 + self.pss_i % 3]
        self.pss_i += 1
        return b

    def tf(self):
        b = self.tmpf[self.tf_i % len(self.tmpf)]
        self.tf_i += 1
        return b

    def tb(self):
        b = self.tmpb[self.tb_i % len(self.tmpb)]
        self.tb_i += 1
        return b

    def dma(self, E, out, in_, reads, writes, dsem, **kw):
        return self.emit(E, lambda: E.h.dma_start(out=out, in_=in_, **kw), reads, writes, dsem=dsem)

    def act(self, out, in_, func, reads, writes, **kw):
        return self.emit(self.ACT, lambda: self.nc.scalar.activation(out=out, in_=in_, func=func, **kw), reads, writes)

    def mm_group(self, pb, pairs, reads, out_ap=None):
        out_ap = pb.t[:] if out_ap is None else out_ap
        n = len(pairs)

        def fn():
            ins = None
            for i, (l, r) in enumerate(pairs):
                ins = self.nc.tensor.matmul(out_ap, lhsT=l, rhs=r, start=(i == 0), stop=(i == n - 1))
            return ins

        return self.emit(self.PE, fn, reads, [(pb, None)])

    def flush_deferred(self):
        d, self.deferred = self.deferred, []
        for fn in d:
            fn()

    def defer(self, fn, after=2):
        self.deferred.append(fn)
        self.defer_at = self.unit_ctr + after

    def load_unit(self, name, r0, nr, c0, ncol):
        if self.deferred and self.unit_ctr >= self.defer_at:
            self.flush_deferred()
        self.unit_ctr += 1
        slot = self.ring[self.ring_i % RING]
        self.ring_i += 1
        src = self.scr[name]
        n = nr * ncol
        assert n <= SLOT_ELEMS
        out = slot.t[:, 0:n].rearrange("p (a b) -> p a b", a=nr)
        self.dma(self.SP, out, src.t[:, r0:r0 + nr, c0:c0 + ncol], [(src, None)], [(slot, None)], slot.dsem)
        return slot

    def get_unit(self, name, r0, nr, c0, ncol):
        key = (name, r0, nr, c0, ncol)
        ent = self.ucache.get(key)
        if ent is not None and ent[0].gen == ent[1]:
            return ent[0]
        slot = self.load_unit(name, r0, nr, c0, ncol)
        slot.gen += 1
        self.ucache[key] = (slot, slot.gen)
        return slot

    def prepass(self):
        nc = self.nc
        W = self.w_in
        G = self.POOL

        hist = []

        def cast(name, src_ap):
            b = self.scr[name]
            self.dma(G, b.t, src_ap, [], [(b, None)], b.dsem)

        rr = lambda ap: ap.rearrange("(kc p) n -> p kc n", p=128)
        with nc.allow_non_contiguous_dma(reason="one-time tiny parameter loads"):
            gl = [("ffn1_norm", 0), ("ffn1_norm", 1), ("mix_norm", 0), ("mix_norm", 1), ("ffn2_norm", 0),
                  ("ffn2_norm", 1), ("ple_norm", 0), ("ple_norm", 1)]
            for i, (k, l) in enumerate(gl):
                self.dma(self.SP, self.gains.t[:, i, :], W[k][l].rearrange("(kc p) -> p kc", p=128), [],
                         [(self.gains, None)], self.gains.dsem)
            self.dma(self.SP, self.gqk.t[:, 0:1], W["attn_q_norm"][0].rearrange("(p o) -> p o", o=1), [],
                     [(self.gqk, None)], self.gqk.dsem)
            self.dma(self.SP, self.gqk.t[:, 1:2], W["attn_k_norm"][0].rearrange("(p o) -> p o", o=1), [],
                     [(self.gqk, None)], self.gqk.dsem)
        self.dma(self.SP, self.ident.t[:], W["c_ident"], [], [(self.ident, None)], self.ident.dsem)
        self.dma(self.SP, self.pswap_f.t[:], W["c_pswap"], [], [(self.pswap_f, None)], self.pswap_f.dsem)
        self.dma(self.SP, self.vg.t[0:1, :], W["gmlp_v_norm"], [], [(self.vg, None)], self.vg.dsem)
        cast("g10", rr(W["ffn1_w_gate"][0]))
        cast("u10", rr(W["ffn1_w_up"][0]))
        cast("d10", rr(W["ffn1_w_down"][0]))
        cast("qkv", rr(W["attn_w_qkv"][0]))
        cast("pp0", rr(W["ple_w_proj"][0]))
        cast("pp1", rr(W["ple_w_proj"][1]))
        cast("wo", rr(W["attn_w_o"][0]))
        cast("g20", rr(W["ffn2_w_gate"][0]))
        cast("u20", rr(W["ffn2_w_up"][0]))
        cast("d20", rr(W["ffn2_w_down"][0]))
        cast("pg0", rr(W["ple_w_gate"][0]))
        cast("g11", rr(W["ffn1_w_gate"][1]))
        cast("u11", rr(W["ffn1_w_up"][1]))
        cast("d11", rr(W["ffn1_w_down"][1]))
        cast("win", rr(W["gmlp_w_in"][0]))
        cast("wout", rr(W["gmlp_w_out"][0]))
        cast("g21", rr(W["ffn2_w_gate"][1]))
        cast("u21", rr(W["ffn2_w_up"][1]))
        cast("d21", rr(W["ffn2_w_down"][1]))
        cast("pg1", rr(W["ple_w_gate"][1]))
        V = self.DVE
        self.emit(V, lambda: nc.vector.memset(self.onesD.t[:], 1.0 / D), [], [(self.onesD, None)])
        self.emit(V, lambda: nc.vector.memset(self.ones128.t[:], 1.0 / HD), [], [(self.ones128, None)])
        self.emit(V, lambda: nc.vector.memset(self.ones1.t[:], 1.0), [], [(self.ones1, None)])
        self.emit(V, lambda: nc.vector.memset(self.onesf.t[:], 1.0), [], [(self.onesf, None)])
        self.emit(V, lambda: nc.vector.tensor_copy(out=self.pswap.t[:], in_=self.pswap_f.t[:]),
                  [(self.pswap_f, None)], [(self.pswap, None)])
        for half in range(2):
            pb = self.ps_next()
            self.mm_group(pb, [(self.onesf.t[0:1, :], self.vg.t[0:1, half * T:(half + 1) * T])],
                          [(self.onesf, None), (self.vg, None)])
            self.emit(V, lambda: nc.vector.tensor_copy(out=self.vnbc.t[:, half * T:(half + 1) * T], in_=pb.t[:]),
                      [(pb, None)], [(self.vnbc, None)])
        y0, y1 = self.yout[0], self.yout[1]
        self.dma(self.SP, y0.t[0:1, :], W["gmlp_b_s"].rearrange("o g p -> o (g p)"), [], [(y0, None)], y0.dsem)
        self.emit(V, lambda: nc.vector.tensor_copy(out=self.bs2.t[0:1, :], in_=y0.t[0:1, :]), [(y0, None)], [(self.bs2, None)])
        self.emit(V, lambda: nc.vector.tensor_copy(out=y1.t[0:1, :], in_=self.bs2.t[0:1, :]), [(self.bs2, None)], [(y1, None)])
        self.emit(V, lambda: nc.vector.tensor_tensor(out=y1.t[0:1, :], in0=y0.t[0:1, :], in1=y1.t[0:1, :], op=ALU.subtract),
                  [(y0, None), (y1, None)], [(y1, None)])
        lo16 = self.hid.t[0:1, 0:2, :].rearrange("p a b -> p (a b)")
        self.emit(V, lambda: nc.vector.tensor_copy(out=lo16, in_=y1.t[0:1, :]), [(y1, None)], [(self.hid, [0, 1, 2, 3])])
        self.dma(self.SP, self.bs2.t[1:2, :], lo16, [(self.hid, [0, 1, 2, 3])], [(self.bs2, None)], self.bs2.dsem)
        for g in range(8):
            xb = self.xin[g % 2]
            self.dma(self.SP, xb.t[:, 0:128], W["gmlp_w_s"][0, g], [], [(xb, None)], xb.dsem)
            pb = self.ps_next()
            self.emit(self.PE, lambda: nc.tensor.transpose(out=pb.t[:, 0:128], in_=xb.t[:, 0:128], identity=self.ident.t[:]),
                      [(xb, None), (self.ident, None)], [(pb, None)])
            self.emit(V, lambda: nc.vector.tensor_copy(out=self.wsT.t[:, g, :], in_=pb.t[:, 0:128]),
                      [(pb, None)], [(self.wsT, None)])

    @staticmethod
    def cs(half):
        return slice(half * HW, (half + 1) * HW)

    @staticmethod
    def hs(n, half):
        return [i * 2 + half for i in range(n)]

    def norm_S(self, half):
        c = self.cs(half)
        self.act(self.sqb.t[:, :, c], self.hw.t[:, :, c], AF.Square, [(self.hw, self.hs(KC, half))], [(self.sqb, half)])

    def norm_MN(self, half, gi):
        nc = self.nc
        c = self.cs(half)
        pn = self.ps_next()
        self.mm_group(pn, [(self.onesD.t[:], self.sqb.t[:, kc, c]) for kc in range(KC)],
                      [(self.sqb, half), (self.onesD, None)], out_ap=pn.t[:, 0:HW])
        self.act(self.sdb.t[:, c], pn.t[:, 0:HW], AF.Ln, [(pn, None)], [(self.sdb, half)], bias=EPS)
        self.act(self.rsb.t[:, c], self.sdb.t[:, c], AF.Exp, [(self.sdb, half)], [(self.rsb, half)], scale=-0.5)
        for kc in range(KC):
            self.emit(self.DVE, lambda: nc.vector.scalar_tensor_tensor(
                out=self.hn.t[:, kc, c], in0=self.hw.t[:, kc, c], scalar=self.gains.t[:, gi, kc:kc + 1], in1=self.rsb.t[:, c],
                op0=ALU.mult, op1=ALU.mult),
                [(self.hw, kc * 2 + half), (self.gains, None), (self.rsb, half)], [(self.hn, kc * 2 + half)])

    def transition(self, tail, nb, gi, head_first=None):
        for b in range(nb - 2):
            tail(0, b)
            tail(1, b)
        tail(0, nb - 2)
        tail(0, nb - 1)
        self.norm_S(0)
        tail(1, nb - 2)
        self.norm_MN(0, gi)
        tail(1, nb - 1)
        self.norm_S(1)
        if head_first is not None:
            head_first()
        self.norm_MN(1, gi)

    def proj_group(self, slot, cw, coff, half=None):
        pb = self.ps_next()
        if half is None:
            pairs = [(slot.t[:, kc * cw + coff: kc * cw + coff + 128], self.hn.t[:, kc, :]) for kc in range(KC)]
            self.mm_group(pb, pairs, [(slot, None), (self.hn, None)])
        else:
            c = self.cs(half)
            pairs = [(slot.t[:, kc * cw + coff: kc * cw + coff + 128], self.hn.t[:, kc, c]) for kc in range(KC)]
            self.mm_group(pb, pairs, [(slot, None), (self.hn, self.hs(KC, half))], out_ap=pb.t[:, 0:HW])
        return pb

    def ffn_gu_block(self, l, which, ub, half):
        nc = self.nc
        f = which + 1
        c0 = ub * 512
        cw = min(512, DFF - c0)
        sg = self.get_unit(f"g{f}{l}", 0, KC, c0, cw)
        su = self.get_unit(f"u{f}{l}", 0, KC, c0, cw)
        c = self.cs(half)
        for jj in range(cw // 128):
            j = ub * 4 + jj
            pg = self.proj_group(sg, cw, jj * 128, half)
            pu = self.proj_group(su, cw, jj * 128, half)
            tg = self.tf()
            self.act(tg.t[:, 0:HW], pg.t[:, 0:HW], AF.Silu, [(pg, None)], [(tg, None)])
            self.emit(self.DVE, lambda: nc.vector.tensor_tensor(out=self.hid.t[:, j, c], in0=pu.t[:, 0:HW], in1=tg.t[:, 0:HW],
                                                                op=ALU.mult),
                      [(pu, None), (tg, None)], [(self.hid, 2 * j + half)])

    def ffn_down_block(self, l, which, half, b):
        nc = self.nc
        dname = f"d{which + 1}{l}"
        c = self.cs(half)
        mp, mms = [(0, (0, 1)), (1, (0, 1)), (2, (0, 1)), (3, (0,)), (3, (1,))][b]
        sd = [self.get_unit(dname, jh * 11, 11, mp * 256, 256) for jh in range(2)]
        for mm in mms:
            m = mp * 2 + mm
            po = self.ps_next()
            pairs = [(sd[j // 11].t[:, (j % 11) * 256 + mm * 128:(j % 11) * 256 + mm * 128 + 128], self.hid.t[:, j, c])
                     for j in range(NJ)]
            self.mm_group(po, pairs, [(sd[0], None), (sd[1], None), (self.hid, self.hs(NJ, half))], out_ap=po.t[:, 0:HW])
            self.emit(self.DVE, lambda: nc.vector.scalar_tensor_tensor(
                out=self.hw.t[:, m, c], in0=po.t[:, 0:HW], scalar=0.5, in1=self.hw.t[:, m, c], op0=ALU.mult, op1=ALU.add),
                [(po, None), (self.hw, m * 2 + half)], [(self.hw, m * 2 + half)])

    def ffn_body(self, l, which, first_done=True):
        order = [(0, 0), (1, 0), (0, 1), (1, 1)] + [(ub, h) for ub in range(2, 6) for h in range(2)]
        if first_done:
            order = order[1:]
        for ub, h in order:
            self.ffn_gu_block(l, which, ub, h)

    def ffn_head(self, l, which):
        return lambda: self.ffn_gu_block(l, which, 0, 0)

    def ffn_tail(self, l, which):
        return (lambda half, b: self.ffn_down_block(l, which, half, b)), 5

    def qk_post_stages(self, n, get_pq, gcol, rp, out_ap, out_dep):
        nc = self.nc
        st = {}

        def s1(i):
            pq = get_pq(i)
            sq = self.tb()
            self.act(sq.t[:], pq.t[:], AF.Square, [(pq, None)], [(sq, None)])
            pn = self.ps_small()
            self.mm_group(pn, [(self.ones128.t[:], sq.t[:])], [(sq, None), (self.ones128, None)])
            st[i] = dict(pq=pq, pn=pn)

        def s2(i):
            d = st[i]
            sd = self.tf()
            self.act(sd.t[:], d["pn"].t[:], AF.Ln, [(d["pn"], None)], [(sd, None)], bias=EPS)
            rs = self.tf()
            self.act(rs.t[:], sd.t[:], AF.Exp, [(sd, None)], [(rs, None)], scale=-0.5)
            kn = self.tb()
            self.emit(self.DVE, lambda: nc.vector.scalar_tensor_tensor(
                out=kn.t[:], in0=d["pq"].t[:], scalar=self.gqk.t[:, gcol:gcol + 1], in1=rs.t[:], op0=ALU.mult, op1=ALU.mult),
                [(d["pq"], None), (self.gqk, None), (rs, None)], [(kn, None)])
            pr = self.ps_small()
            self.mm_group(pr, [(self.pswap.t[:], kn.t[:])], [(self.pswap, None), (kn, None)])
            d["kn"] = kn
            d["pr"] = pr

        def s3(i):
            d = st.pop(i)
            t1 = self.tf()
            self.emit(self.DVE, lambda: nc.vector.tensor_tensor(out=t1.t[:], in0=d["kn"].t[:], in1=rp.t[:, 0, :], op=ALU.mult),
                      [(d["kn"], None), (rp, None)], [(t1, None)])
            t2 = self.tf()
            self.emit(self.DVE, lambda: nc.vector.tensor_tensor(out=t2.t[:], in0=d["pr"].t[:], in1=rp.t[:, 1, :], op=ALU.mult),
                      [(d["pr"], None), (rp, None)], [(t2, None)])
            ob, osl = out_dep(i)
            self.emit(self.DVE, lambda: nc.vector.tensor_tensor(out=out_ap(i), in0=t1.t[:], in1=t2.t[:], op=ALU.add),
                      [(t1, None), (t2, None)], [(ob, osl)])

        return s1, s2, s3

    @staticmethod
    def skew(n, stages):
        for step in range(n + len(stages) - 1):
            for si, f in enumerate(stages):
                i = step - si
                if 0 <= i < n:
                    f(i)

    def load_rope(self, tile_in_seq):
        rp = self.rope
        W = self.w_in
        t0 = tile_in_seq * T
        self.dma(self.SP, rp.t[:, 0, :], W["c_cos"][:, t0:t0 + T], [], [(rp, None)], rp.dsem)
        self.dma(self.SP, rp.t[:, 1, :], W["c_sin"][:, t0:t0 + T], [], [(rp, None)], rp.dsem)
        return rp

    def x_chunk(self, xseq, ti, c):
        nc = self.nc
        xb = self.xin[c % 2]
        r0 = ti * T + c * 128
        half = c // 2
        self.dma(self.SP, xb.t[:], xseq[r0:r0 + 128, :], [], [(xb, None)], xb.dsem)
        for q4 in range(2):
            pb = self.ps_next()

            def fn():
                ins = None
                for q in range(4):
                    kc = q4 * 4 + q
                    ins = nc.tensor.transpose(out=pb.t[:, q * 128:(q + 1) * 128], in_=xb.t[:, kc * 128:(kc + 1) * 128],
                                              identity=self.ident.t[:])
                return ins

            self.emit(self.PE, fn, [(xb, None), (self.ident, None)], [(pb, None)])
            dst = self.hw.t[:, q4 * 4:q4 * 4 + 4, c * 128:(c + 1) * 128]
            src = pb.t[:].rearrange("p (a b) -> p a b", a=4)
            wr = [(self.hw, [(q4 * 4 + q) * 2 + half for q in range(4)])]
            if (c + q4) % 2 == 0:
                self.emit(self.DVE, lambda: nc.vector.tensor_copy(out=dst, in_=src), [(pb, None)], wr)
            else:
                self.act(dst, src, AF.Copy, [(pb, None)], wr)

    def tile_A(self, xseq, ti, gt):
        nc = self.nc
        rp = self.load_rope(ti)
        self.transition(lambda half, b: self.x_chunk(xseq, ti, half * 2 + b), 2, 0, self.ffn_head(0, 0))
        self.ffn_body(0, 0)
        tail, nb = self.ffn_tail(0, 0)
        self.transition(tail, nb, 2, None)
        dst = self.scr_h_t[gt].rearrange("p (a b) -> p a b", a=KC)
        self.dma(self.ACT, dst, self.hw.t[:], [(self.hw, None)], [(self.scr_h, gt)], self.hw.dsem)
        sk = self.get_unit("qkv", 0, KC, 1024, 256)
        sv = self.get_unit("qkv", 0, KC, 1280, 256)
        pqs = {}

        def s0(i):
            pqs[i] = self.proj_group(sk, 256, i * 128)

        s1, s2, s3 = self.qk_post_stages(
            NKV, lambda i: pqs[i], 1, rp,
            lambda i: self.KT.t[:, i, ti * T:(ti + 1) * T], lambda i: (self.KT, i * (SMAX // T) + ti))
        self.skew(NKV, [s0, s1, s2, s3])
        for c in range(4):
            pb = self.ps_next()
            pairs = [(self.hn.t[:, kc, c * 128:(c + 1) * 128], sv.t[:, kc * 256:(kc + 1) * 256]) for kc in range(KC)]
            self.mm_group(pb, pairs, [(sv, None), (self.hn, self.hs(KC, c // 2))], out_ap=pb.t[:, 0:256])
            ch = ti * 4 + c
            self.act(self.V.t[:, ch, :], pb.t[:, 0:256], AF.Copy, [(pb, None)], [(self.V, ch)])

    def attention_core(self, S, ti, rp):
        nc = self.nc
        squ = [self.get_unit("qkv", 0, KC, ub * 512, 512) for ub in range(2)]
        pqs = {}

        def s0(h):
            pqs[h] = self.proj_group(squ[h // 4], 512, (h % 4) * 128)

        s1, s2, s3 = self.qk_post_stages(NQH, lambda h: pqs.pop(h), 0, rp, lambda h: self.hid.t[:, h, :],
                                         lambda h: (self.hid, [2 * h, 2 * h + 1]))
        self.skew(NQH, [s0, s1, s2, s3])
        nkc = S // 128
        scale = float(HD) ** -0.5
        for h in range(NQH):
            kvh = h // 4
            Ob, Db = self.psb[4 + 2 * (h % 2)], self.psb[5 + 2 * (h % 2)]
            st = {}

            def f0(kc):
                ps = self.ps_next()
                self.mm_group(ps, [(self.KT.t[:, kvh, kc * 128:(kc + 1) * 128], self.hid.t[:, h, :])],
                              [(self.KT, kvh * (SMAX // T) + kc // 4), (self.hid, [2 * h, 2 * h + 1])])
                pe = self.pexp[kc % len(self.pexp)]
                self.act(pe.t[:], ps.t[:], AF.Exp, [(ps, None)], [(pe, None)], scale=scale)
                st[kc] = pe

            def f1(kc):
                pe = st.pop(kc)
                self.emit(self.PE, lambda: nc.tensor.matmul(Ob.t[:], lhsT=self.V.t[:, kc, kvh * HD:(kvh + 1) * HD], rhs=pe.t[:],
                                                            start=(kc == 0), stop=(kc == nkc - 1)),
                          [(self.V, kc), (pe, None)], [(Ob, None)])
                self.emit(self.PE, lambda: nc.tensor.matmul(Db.t[:], lhsT=self.ones1.t[:], rhs=pe.t[:],
                                                            start=(kc == 0), stop=(kc == nkc - 1)),
                          [(self.ones1, None), (pe, None)], [(Db, None)])

            self.skew(nkc, [f0, lambda kc: None, f1])
            rd = self.tf()
            self.emit(self.DVE, lambda: nc.vector.reciprocal(out=rd.t[:], in_=Db.t[:]), [(Db, None)], [(rd, None)])
            self.emit(self.DVE, lambda: nc.vector.tensor_tensor(out=self.hid.t[:, 8 + h, :], in0=Ob.t[:], in1=rd.t[:], op=ALU.mult),
                      [(Ob, None), (rd, None)], [(self.hid, [2 * (8 + h), 2 * (8 + h) + 1])])

    def oproj_block(self, half, ub):
        nc = self.nc
        c = self.cs(half)
        so = self.get_unit("wo", 0, KC, ub * 512, 512)
        for mm in range(4):
            m = ub * 4 + mm
            po = self.ps_next()
            pairs = [(so.t[:, h * 512 + mm * 128:h * 512 + mm * 128 + 128], self.hid.t[:, 8 + h, c]) for h in range(NQH)]
            self.mm_group(po, pairs, [(so, None), (self.hid, [2 * (8 + h) + half for h in range(NQH)])], out_ap=po.t[:, 0:HW])
            self.emit(self.DVE, lambda: nc.vector.tensor_tensor(out=self.hw.t[:, m, c], in0=po.t[:, 0:HW], in1=self.hw.t[:, m, c],
                                                                op=ALU.add),
                      [(po, None), (self.hw, m * 2 + half)], [(self.hw, m * 2 + half)])

    def ple_prep(self, l, pseq_l, ti):
        nc = self.nc
        pb_in = self.pin
        pin_v = pb_in.t[:].rearrange("p (c d) -> p c d", c=4)
        self.dma(self.SP, pin_v, pseq_l[ti * T:(ti + 1) * T, :].rearrange("(c p) d -> p c d", p=128), [],
                 [(pb_in, None)], pb_in.dsem)
        if not self.ppw_loaded:
            self.ppw_loaded = True
            for ll in range(2):
                src = self.scr[f"pp{ll}"]
                self.dma(self.SP, self.ppw.t[:, ll, :].rearrange("p (a b) -> p a b", a=2), src.t, [(src, None)],
                         [(self.ppw, ll)], self.ppw.dsem)
        for k2 in range(2):
            pb = self.ps_next()

            def fn():
                ins = None
                for c in range(4):
                    ins = nc.tensor.transpose(out=pb.t[:, c * 128:(c + 1) * 128], in_=pin_v[:, c, k2 * 128:(k2 + 1) * 128],
                                              identity=self.ident.t[:])
                return ins

            self.emit(self.PE, fn, [(pb_in, None), (self.ident, None)], [(pb, None)])
            self.act(self.pT.t[:, k2, :], pb.t[:], AF.Copy, [(pb, None)], [(self.pT, k2)])

    def ple_block(self, l, half, ub):
        nc = self.nc
        c = self.cs(half)
        sg = self.get_unit(f"pg{l}", 0, KC, ub * 512, 512)
        for mm in range(4):
            m = ub * 4 + mm
            pg = self.proj_group(sg, 512, mm * 128, half)
            tg = self.tf()
            self.act(tg.t[:, 0:HW], pg.t[:, 0:HW], AF.Sigmoid, [(pg, None)], [(tg, None)])
            pp = self.ps_next()
            pairs = [(self.ppw.t[:, l, k2 * D + m * 128:k2 * D + m * 128 + 128], self.pT.t[:, k2, c]) for k2 in range(2)]
            self.mm_group(pp, pairs, [(self.ppw, l), (self.pT, None)], out_ap=pp.t[:, 0:HW])
            t2 = self.tf()
            self.emit(self.DVE, lambda: nc.vector.tensor_tensor(out=t2.t[:, 0:HW], in0=pp.t[:, 0:HW], in1=tg.t[:, 0:HW], op=ALU.mult),
                      [(pp, None), (tg, None)], [(t2, None)])
            self.emit(self.POOL, lambda: nc.gpsimd.tensor_tensor(out=self.hw.t[:, m, c], in0=t2.t[:, 0:HW], in1=self.hw.t[:, m, c],
                                                                 op=ALU.add),
                      [(t2, None), (self.hw, m * 2 + half)], [(self.hw, m * 2 + half)])

    def gmlp_u_block(self, ub, half):
        su = self.get_unit("win", 0, KC, ub * 512, 512)
        c = self.cs(half)
        for mm in range(4):
            m = ub * 4 + mm
            pu = self.proj_group(su, 512, mm * 128, half)
            self.act(self.uT.t[:, m, c], pu.t[:, 0:HW], AF.Gelu, [(pu, None)], [(self.uT, m * 2 + half)])

    def gmlp_v_chunk(self, c):
        nc = self.nc
        half = c // 2
        svv = [self.get_unit("win", 0, KC, D + hf * 512, 512) for hf in range(2)]
        stt = self.stat[c % 2]
        pbs = []
        for hf in range(2):
            pb = self.ps_next()
            pairs = [(self.hn.t[:, kc, c * 128:(c + 1) * 128], svv[hf].t[:, kc * 512:(kc + 1) * 512]) for kc in range(KC)]
            self.mm_group(pb, pairs, [(svv[hf], None), (self.hn, self.hs(KC, half))])
            pbs.append(pb)
        for hf in range(2):
            self.act(self.vg.t[:, hf * 512:(hf + 1) * 512], pbs[hf].t[:], AF.Gelu, [(pbs[hf], None)], [(self.vg, None)])
        for hf in range(2):
            junk = self.tf()
            self.act(junk.t[:], self.vg.t[:, hf * 512:(hf + 1) * 512], AF.Square, [(self.vg, None)],
                     [(junk, None), (stt, None)], accum_out=stt.t[:, hf:hf + 1])
        self.emit(self.DVE, lambda: nc.vector.tensor_tensor(out=stt.t[:, 2:3], in0=stt.t[:, 0:1], in1=stt.t[:, 1:2], op=ALU.add),
                  [(stt, None)], [(stt, None)])
        self.act(stt.t[:, 3:4], stt.t[:, 2:3], AF.Ln, [(stt, None)], [(stt, None)], bias=EPS, scale=1.0 / D)
        self.act(stt.t[:, 2:3], stt.t[:, 3:4], AF.Exp, [(stt, None)], [(stt, None)], scale=-0.5)
        vdst = self.hid.t[:, 2 * c:2 * c + 2, :].rearrange("p a b -> p (a b)")
        self.emit(self.DVE, lambda: nc.vector.scalar_tensor_tensor(
            out=vdst, in0=self.vg.t[:], scalar=stt.t[:, 2:3], in1=self.vnbc.t[:], op0=ALU.mult, op1=ALU.mult),
            [(self.vg, None), (stt, None), (self.vnbc, None)], [(self.hid, [4 * c, 4 * c + 1, 4 * c + 2, 4 * c + 3])])

    def gmlp_spatial(self, g, half):
        nc = self.nc
        c_ = self.cs(half)
        pb = self.ps_next()

        def fn():
            ins = None
            for cc in range(2):
                c = half * 2 + cc
                vsl = self.hid.t[:, 2 * c + g // 4, (g % 4) * 128:(g % 4) * 128 + 128]
                nc.tensor.matmul(pb.t[:, cc * 128:(cc + 1) * 128], lhsT=vsl, rhs=self.wsT.t[:, g, :], start=True, stop=False)
                ins = nc.tensor.matmul(pb.t[:, cc * 128:(cc + 1) * 128], lhsT=self.ones1.t[0:2, :],
                                       rhs=self.bs2.t[0:2, g * 128:(g + 1) * 128], start=False, stop=True)
            return ins

        rd = [(self.hid, [2 * (2 * (half * 2 + cc) + g // 4) + hh for cc in range(2) for hh in range(2)]),
              (self.wsT, None), (self.ones1, None), (self.bs2, None)]
        self.emit(self.PE, fn, rd, [(pb, None)])
        self.emit(self.DVE, lambda: nc.vector.tensor_tensor(out=self.hid.t[:, 8 + g, c_], in0=pb.t[:, 0:HW], in1=self.uT.t[:, g, c_],
                                                            op=ALU.mult),
                  [(pb, None), (self.uT, g * 2 + half)], [(self.hid, 2 * (8 + g) + half)])

    def gmlp_out_block(self, half, ub):
        nc = self.nc
        c = self.cs(half)
        so = self.get_unit("wout", 0, KC, ub * 512, 512)
        for mm in range(4):
            m = ub * 4 + mm
            po = self.ps_next()
            pairs = [(so.t[:, kc * 512 + mm * 128:kc * 512 + mm * 128 + 128], self.hid.t[:, 8 + kc, c]) for kc in range(KC)]
            self.mm_group(po, pairs, [(so, None), (self.hid, [2 * (8 + kc) + half for kc in range(KC)])], out_ap=po.t[:, 0:HW])
            self.emit(self.DVE, lambda: nc.vector.tensor_tensor(out=self.hw.t[:, m, c], in0=po.t[:, 0:HW], in1=self.hw.t[:, m, c],
                                                                op=ALU.add),
                      [(po, None), (self.hw, m * 2 + half)], [(self.hw, m * 2 + half)])

    def out_chunk(self, yseq, ti, c):
        nc = self.nc
        half = c // 2
        yb = self.yout[c % 2]
        for q4 in range(2):
            pb = self.ps_next()

            def fn():
                ins = None
                for q in range(4):
                    kc = q4 * 4 + q
                    ins = nc.tensor.transpose(out=pb.t[:, q * 128:(q + 1) * 128], in_=self.hw.t[:, kc, c * 128:(c + 1) * 128],
                                              identity=self.ident.t[:])
                return ins

            self.emit(self.PE, fn, [(self.hw, [(q4 * 4 + q) * 2 + half for q in range(4)]), (self.ident, None)], [(pb, None)])
            if q4 == 0:
                self.emit(self.DVE, lambda: nc.vector.tensor_copy(out=yb.t[:, 0:512], in_=pb.t[:]), [(pb, None)], [(yb, None)])
            else:
                self.act(yb.t[:, 512:1024], pb.t[:], AF.Copy, [(pb, None)], [(yb, None)])
        r0 = ti * T + c * 128
        self.dma(self.ACT, yseq[r0:r0 + 128, :], yb.t[:], [(yb, None)], [], yb.dsem)

    def prefetch_next_h(self):
        if self.next_h is None:
            return
        gt = self.next_h
        src = self.scr_h_t[gt].rearrange("p (a b) -> p a b", a=KC)
        self.dma(self.SP, self.uT.t[:], src, [(self.scr_h, gt)], [(self.uT, None)], self.uT.dsem)
        self.prefetched = gt

    def tile_B(self, S, yseq, pseq, ti, gt, stop_after=99, has_next=False):
        rp = self.load_rope(ti)
        if getattr(self, "prefetched", None) == gt:
            self.hw, self.uT = self.uT, self.hw
        else:
            src = self.scr_h_t[gt].rearrange("p (a b) -> p a b", a=KC)
            self.dma(self.SP, self.hw.t[:], src, [(self.scr_h, gt)], [(self.hw, None)], self.hw.dsem)
        self.next_h = gt + 1 if has_next else None
        self.norm_S(0)
        self.norm_S(1)
        self.norm_MN(0, 2)
        self.norm_MN(1, 2)
        self.attention_core(S, ti, rp)
        self.transition(lambda half, b: self.oproj_block(half, b), 2, 4, self.ffn_head(0, 1))
        self.ffn_body(0, 1)
        tail, nb = self.ffn_tail(0, 1)
        self.ple_prep(0, pseq[0], ti)
        self.transition(tail, nb, 6, lambda: self.ple_block(0, 0, 0))
        self.ple_block(0, 0, 1)
        self.norm_S(0)
        self.ple_block(0, 1, 0)
        self.norm_MN(0, 1)
        self.ple_block(0, 1, 1)
        self.norm_S(1)
        self.ffn_gu_block(1, 0, 0, 0)
        self.norm_MN(1, 1)
        self.ffn_body(1, 0)
        tail, nb = self.ffn_tail(1, 0)
        self.transition(tail, nb, 3, lambda: self.gmlp_u_block(0, 0))
        self.gmlp_v_chunk(0)
        self.gmlp_v_chunk(1)
        self.gmlp_u_block(1, 0)
        self.gmlp_v_chunk(2)
        self.gmlp_v_chunk(3)
        self.gmlp_u_block(0, 1)
        self.gmlp_u_block(1, 1)
        for ub in range(2):
            self.get_unit("wout", 0, KC, ub * 512, 512)
        for g in range(8):
            self.gmlp_spatial(g, 0)
        for g in range(8):
            self.gmlp_spatial(g, 1)
        self.prefetch_next_h()
        self.transition(lambda half, b: self.gmlp_out_block(half, b), 2, 5, self.ffn_head(1, 1))
        self.ffn_body(1, 1)
        tail, nb = self.ffn_tail(1, 1)
        self.ple_prep(1, pseq[1], ti)
        self.transition(tail, nb, 7, lambda: self.ple_block(1, 0, 0))
        self.ple_block(1, 0, 1)
        self.out_chunk(yseq, ti, 0)
        self.ple_block(1, 1, 0)
        self.out_chunk(yseq, ti, 1)
        self.ple_block(1, 1, 1)
        self.out_chunk(yseq, ti, 2)
        self.out_chunk(yseq, ti, 3)

    def build(self, stop_after=99):
        self.prepass()
        seqs = []
        for b in range(self.nps):
            seqs.append((self.sp_len, self.xp[b], self.yp[b], [self.pp[0, b], self.pp[1, b]]))
        for b in range(self.nss):
            seqs.append((self.ss_len, self.xs[b], self.ys[b], [self.psm[0, b], self.psm[1, b]]))
        gt0 = 0
        for (S, xseq, yseq, pseq) in seqs:
            nt = S // T
            for ti in range(nt):
                self.tile_A(xseq, ti, gt0 + ti)
            for ti in range(nt):
                self.tile_B(S, yseq, pseq, ti, gt0 + ti, stop_after=stop_after, has_next=(ti + 1 < nt))
            gt0 += nt
        self.flush_deferred()
        for yb in self.yout:
            if yb.dsem.count:
                self.SP.h.wait_ge(yb.dsem.h, yb.dsem.count)
        return self.nc


def _consts():
    half = HD // 2
    inv = (10000.0 ** (-np.arange(0, half, 2, dtype=np.float32) / np.float32(half))).astype(np.float32)
    t = np.arange(SMAX)
    row = (t // 64).astype(np.float32)
    col = (t % 64).astype(np.float32)
    ang_r = row[:, None] * inv[None, :]
    ang_c = col[:, None] * inv[None, :]
    cos = np.zeros((128, SMAX), np.float32)
    sin = np.zeros((128, SMAX), np.float32)
    for d in range(128):
        a = (ang_r if d < 64 else ang_c)[:, d % 32]
        cos[d] = np.cos(a)
        sgn = -1.0 if (d % 64) < 32 else 1.0
        sin[d] = sgn * np.sin(a)
    ident = np.eye(128, dtype=np.float32)
    ps = np.zeros((128, 128), np.float32)
    for d in range(128):
        partner = d + 32 if (d % 64) < 32 else d - 32
        ps[partner, d] = 1.0
    return dict(c_cos=cos, c_sin=sin, c_ident=ident, c_pswap=ps)


_WKEYS = ["ffn1_norm", "ffn1_w_gate", "ffn1_w_up", "ffn1_w_down", "mix_norm", "attn_w_qkv", "attn_q_norm", "attn_k_norm",
          "attn_w_o", "gmlp_w_in", "gmlp_v_norm", "gmlp_w_s", "gmlp_b_s", "gmlp_w_out", "ffn2_norm", "ffn2_w_gate",
          "ffn2_w_up", "ffn2_w_down", "ple_norm", "ple_w_gate", "ple_w_proj"]


def kernel(**inputs):
    f = lambda a: np.ascontiguousarray(np.asarray(a, dtype=np.float32))
    xp, xs = f(inputs["x_prompt"]), f(inputs["x_sample"])
    pp, ps = f(inputs["p_prompt"]), f(inputs["p_sample"])
    nps, nss = xp.shape[0] // NCORES, xs.shape[0] // NCORES
    nc = Builder(nps, nss).build()
    shared = {k: f(inputs[k]) for k in _WKEYS}
    shared.update(_consts())
    in_maps = []
    for c in range(NCORES):
        m = dict(shared)
        m["xp"] = xp[c * nps:(c + 1) * nps]
        m["xs"] = xs[c * nss:(c + 1) * nss]
        m["pp"] = np.ascontiguousarray(pp[:, c * nps:(c + 1) * nps])
        m["psm"] = np.ascontiguousarray(ps[:, c * nss:(c + 1) * nss])
        in_maps.append(m)
    res = run_bass_kernel_spmd(nc, in_maps, core_ids=list(range(NCORES)))
    yp = np.concatenate([r["yp"] for r in res.results], axis=0).astype(np.float32)
    ys = np.concatenate([r["ys"] for r in res.results], axis=0).astype(np.float32)
    return (yp, ys)
```
